# Optimizing a Trainium2 kernel written in Bass

```python
import math
import numpy as np
import jax
import jax.numpy as jnp
from jax import lax

D_MODEL = 1024
BATCH = 32
SEQ = 2048
DEPTH = 2

N_MEM = 256
HEAD_DIM = 64
MAIN_W = 3 * D_MODEL // 4
N_HEADS = MAIN_W // HEAD_DIM
MEM_HEADS = 4
MEM_W = D_MODEL - MAIN_W
MEM_HEAD_DIM = MEM_W // MEM_HEADS
MIX_W = MAIN_W + MEM_W

NSA_KV_HEADS = 1
NSA_KV_W = NSA_KV_HEADS * HEAD_DIM
CMP_LEN = 32
CMP_STRIDE = 16
CMP_HIDDEN = 4 * HEAD_DIM
SLC_BLOCK = 64
SLC_TOPK = 16
WINDOW = 512
NSA_Q_CHUNK = 64
FORCE_SCORE = 1.0e4

MOBA_BLOCK = 256
MOBA_TOPK = 3
MOBA_Q_CHUNK = 8

REL_BUCKETS = 32
REL_MAX_DIST = 128

D_FF = 2816
CONV_W = 3

N_A = DEPTH // 2
N_B = DEPTH - N_A
ALPHA = (2.0 * DEPTH) ** 0.25
BETA = (8.0 * DEPTH) ** -0.25
LN_EPS = 1e-5
NEG_INF = -1e30
TINY = 1e-30

A_IN_SIZES = (MAIN_W, NSA_KV_W, NSA_KV_W, NSA_KV_W, NSA_KV_W, NSA_KV_W, NSA_KV_W, 3 * N_HEADS, MEM_W)
A_IN = sum(A_IN_SIZES)
B_IN_SIZES = (MAIN_W, MEM_W)
B_IN = sum(B_IN_SIZES)

kernel_name = 'yoco_nsa_moba_hybrid'


def split_cols(h, sizes):
    return jnp.split(h, np.cumsum(sizes)[:-1].tolist(), axis=-1)


def layer_norm(x, g, b):
    xf = x.astype(jnp.float32)
    mu = xf.mean(-1, keepdims=True)
    var = jnp.square(xf - mu).mean(-1, keepdims=True)
    return ((xf - mu) * lax.rsqrt(var + LN_EPS) * g + b).astype(x.dtype)


def rel_bucket(dist):
    n = jnp.maximum(dist, 0)
    max_exact = REL_BUCKETS // 2
    nf = jnp.maximum(n, 1).astype(jnp.float32)
    large = max_exact + (jnp.log(nf / max_exact) / math.log(REL_MAX_DIST / max_exact)
                         * (REL_BUCKETS - max_exact)).astype(jnp.int32)
    large = jnp.minimum(large, REL_BUCKETS - 1)
    return jnp.where(n < max_exact, n, large)


def masked_softmax(logits, mask):
    logits = jnp.where(mask, logits, NEG_INF)
    m = jnp.max(logits, -1, keepdims=True)
    p = jnp.where(mask, jnp.exp(logits - m), 0.0)
    return p / jnp.maximum(p.sum(-1, keepdims=True), TINY)


def cmp_slc_overlap(n_cmp, n_slc):
    start = np.arange(n_cmp) * CMP_STRIDE
    end = start + CMP_LEN
    bs = np.arange(n_slc) * SLC_BLOCK
    m = (start[:, None] < bs[None, :] + SLC_BLOCK) & (end[:, None] > bs[None, :])
    return m.astype(np.float32)


def nsa_compress(k, pe, w1, w2):
    B, S, G, dh = k.shape
    n_cmp = (S - CMP_LEN) // CMP_STRIDE + 1
    idx = (np.arange(n_cmp, dtype=np.int32)[:, None] * CMP_STRIDE
           + np.arange(CMP_LEN, dtype=np.int32)[None, :])
    blocks = k[:, idx] + pe[:, None, :]
    blocks = blocks.transpose(0, 1, 3, 2, 4).reshape(B, n_cmp, G, CMP_LEN * dh)
    return jax.nn.gelu(blocks @ w1) @ w2


def nsa_attention(q, k_cmp, v_cmp, k_slc, v_slc, k_win, v_win, gates, rel_bias):
    B, S, H, dh = q.shape
    G = k_slc.shape[2]
    hpg = H // G
    n_cmp = k_cmp.shape[1]
    n_slc = S // SLC_BLOCK
    top = min(SLC_TOPK, n_slc)
    n_sel = top * SLC_BLOCK
    QC = NSA_Q_CHUNK
    KW = WINDOW + QC
    scale = dh ** -0.5

    cmp_end = jnp.asarray(np.arange(n_cmp) * CMP_STRIDE + CMP_LEN - 1, jnp.int32)
    overlap = jnp.asarray(cmp_slc_overlap(n_cmp, n_slc))
    blk_ids = jnp.arange(n_slc, dtype=jnp.int32)
    offs = jnp.arange(SLC_BLOCK, dtype=jnp.int32)
    k_blk = k_slc.reshape(B, n_slc, SLC_BLOCK, G, dh).transpose(0, 3, 1, 2, 4).reshape(B, G, n_slc, SLC_BLOCK * dh)
    v_blk = v_slc.reshape(B, n_slc, SLC_BLOCK, G, dh).transpose(0, 3, 1, 2, 4).reshape(B, G, n_slc, SLC_BLOCK * dh)
    k_win_pad = jnp.pad(k_win, ((0, 0), (WINDOW, 0), (0, 0), (0, 0)))
    v_win_pad = jnp.pad(v_win, ((0, 0), (WINDOW, 0), (0, 0), (0, 0)))
    table_g = rel_bias.reshape(REL_BUCKETS, G, hpg)
    g_idx = jnp.arange(G)[None, None, :, None]

    def head_bias(bucket):
        return rel_bias[bucket].reshape(bucket.shape + (G, hpg)).transpose(0, 2, 3, 1)

    def chunk(c):
        t0 = c * QC
        t = t0 + jnp.arange(QC, dtype=jnp.int32)
        qc = lax.dynamic_slice_in_dim(q, t0, QC, axis=1).reshape(B, QC, G, hpg, dh)
        gc = lax.dynamic_slice_in_dim(gates, t0, QC, axis=1).reshape(B, QC, G, hpg, 3)

        dist_c = t[:, None] - cmp_end[None, :]
        lg = jnp.einsum('bqgjd,bngd->bqgjn', qc, k_cmp).astype(jnp.float32) * scale + head_bias(rel_bucket(dist_c))
        p_c = masked_softmax(lg, (dist_c >= 0)[None, :, None, None, :])
        o_c = jnp.einsum('bqgjn,bngd->bqgjd', p_c, v_cmp)

        imp = jnp.einsum('bqgjn,ns->bqgs', p_c, overlap)
        cur = t // SLC_BLOCK
        bid = blk_ids[None, :]
        eligible = bid <= cur[:, None]
        forced = (bid == 0) | (bid == cur[:, None]) | (bid == cur[:, None] - 1)
        score = jnp.where(eligible, jnp.where(forced, FORCE_SCORE, 0.0), NEG_INF)[None, :, None, :] + imp
        _, sel = lax.top_k(score, top)
        idx = sel.transpose(0, 2, 1, 3).reshape(B, G, QC * top, 1)
        ks = jnp.take_along_axis(k_blk, idx, axis=2).reshape(B, G, QC, n_sel, dh)
        vs = jnp.take_along_axis(v_blk, idx, axis=2).reshape(B, G, QC, n_sel, dh)
        key_pos = (sel[..., None] * SLC_BLOCK + offs).reshape(B, QC, G, n_sel)
        dist_s = t[None, :, None, None] - key_pos
        bias_s = jnp.swapaxes(table_g[rel_bucket(dist_s), g_idx], -1, -2)
        lg = jnp.einsum('bqgjd,bgqkd->bqgjk', qc, ks).astype(jnp.float32) * scale + bias_s
        p_s = masked_softmax(lg, (dist_s >= 0)[:, :, :, None, :])
        o_s = jnp.einsum('bqgjk,bgqkd->bqgjd', p_s, vs)

        kw = lax.dynamic_slice_in_dim(k_win_pad, t0, KW, axis=1)
        vw = lax.dynamic_slice_in_dim(v_win_pad, t0, KW, axis=1)
        pos_w = t0 - WINDOW + jnp.arange(KW, dtype=jnp.int32)
        dist_w = t[:, None] - pos_w[None, :]
        mask_w = (pos_w[None, :] >= 0) & (dist_w >= 0) & (dist_w < WINDOW)
        lg = jnp.einsum('bqgjd,bkgd->bqgjk', qc, kw).astype(jnp.float32) * scale + head_bias(rel_bucket(dist_w))
        p_w = masked_softmax(lg, mask_w[None, :, None, None, :])
        o_w = jnp.einsum('bqgjk,bkgd->bqgjd', p_w, vw)

        o = gc[..., 0:1] * o_c + gc[..., 1:2] * o_s + gc[..., 2:3] * o_w
        return o.reshape(B, QC, H * dh).astype(q.dtype)

    out = lax.map(chunk, jnp.arange(S // QC, dtype=jnp.int32))
    return out.transpose(1, 0, 2, 3).reshape(B, S, H * dh)


def moba_shared_kv(x, w_kv):
    B, S, _ = x.shape
    k, v = jnp.split(x @ w_kv, 2, axis=-1)
    nb = -(-S // MOBA_BLOCK)
    pad = nb * MOBA_BLOCK - S
    k = jnp.pad(k.reshape(B, S, N_HEADS, HEAD_DIM), ((0, 0), (0, pad), (0, 0), (0, 0)))
    v = jnp.pad(v.reshape(B, S, N_HEADS, HEAD_DIM), ((0, 0), (0, pad), (0, 0), (0, 0)))
    k_blocks = k.reshape(B, nb, MOBA_BLOCK, N_HEADS, HEAD_DIM)
    k_mean = k_blocks.astype(jnp.float32).mean(2).astype(k.dtype)
    k_blk = k_blocks.transpose(0, 3, 1, 2, 4).reshape(B, N_HEADS, nb, MOBA_BLOCK * HEAD_DIM)
    v_blk = v.reshape(B, nb, MOBA_BLOCK, N_HEADS, HEAD_DIM).transpose(0, 3, 1, 2, 4).reshape(B, N_HEADS, nb, MOBA_BLOCK * HEAD_DIM)
    return k_mean, k_blk, v_blk


def moba_attention(q, k_mean, k_blk, v_blk, rel_bias):
    B, S, H, dh = q.shape
    nb = k_mean.shape[1]
    top = min(MOBA_TOPK, nb - 1)
    n_p = top * MOBA_BLOCK
    QC = MOBA_Q_CHUNK
    scale = dh ** -0.5
    blk_ids = jnp.arange(nb, dtype=jnp.int32)
    offs = jnp.arange(MOBA_BLOCK, dtype=jnp.int32)
    h_idx = jnp.arange(H)[None, :, None, None]

    def chunk(c):
        t0 = c * QC
        t = t0 + jnp.arange(QC, dtype=jnp.int32)
        cb = t0 // MOBA_BLOCK
        qc = lax.dynamic_slice_in_dim(q, t0, QC, axis=1)
        ko = lax.dynamic_index_in_dim(k_blk, cb, axis=2, keepdims=False).reshape(B, H, MOBA_BLOCK, dh)
        vo = lax.dynamic_index_in_dim(v_blk, cb, axis=2, keepdims=False).reshape(B, H, MOBA_BLOCK, dh)
        dist_o = t[:, None] - (cb * MOBA_BLOCK + offs)[None, :]
        lg_o = (jnp.einsum('bqhd,bhkd->bhqk', qc, ko).astype(jnp.float32) * scale
                + rel_bias[rel_bucket(dist_o)].transpose(2, 0, 1))
        mask_o = jnp.broadcast_to(dist_o >= 0, lg_o.shape)
        if top == 0:
            p = masked_softmax(lg_o, mask_o)
            o = jnp.einsum('bhqk,bhkd->bqhd', p, vo)
        else:
            gate = jnp.einsum('bqhd,bnhd->bhqn', qc, k_mean).astype(jnp.float32)
            gate = jnp.where(blk_ids < cb, gate, NEG_INF)
            _, sel = lax.top_k(gate, top)
            idx = sel.reshape(B, H, QC * top, 1)
            kp = jnp.take_along_axis(k_blk, idx, axis=2).reshape(B, H, QC, n_p, dh)
            vp = jnp.take_along_axis(v_blk, idx, axis=2).reshape(B, H, QC, n_p, dh)
            pos_p = (sel[..., None] * MOBA_BLOCK + offs).reshape(B, H, QC, n_p)
            lg_p = (jnp.einsum('bqhd,bhqkd->bhqk', qc, kp).astype(jnp.float32) * scale
                    + rel_bias[rel_bucket(t[None, None, :, None] - pos_p), h_idx])
            mask_p = pos_p < cb * MOBA_BLOCK
            p = masked_softmax(jnp.concatenate([lg_p, lg_o], -1), jnp.concatenate([mask_p, mask_o], -1))
            o = (jnp.einsum('bhqk,bhqkd->bqhd', p[..., :n_p], vp)
                 + jnp.einsum('bhqk,bhkd->bqhd', p[..., n_p:], vo))
        return o.reshape(B, QC, H * dh).astype(q.dtype)

    out = lax.map(chunk, jnp.arange(S // QC, dtype=jnp.int32))
    return out.transpose(1, 0, 2, 3).reshape(B, S, H * dh)


def memory_attention(qm, mem, w_mem_kv):
    B, M, _ = mem.shape
    km, vm = jnp.split(mem @ w_mem_kv, 2, axis=-1)
    km = km.reshape(B, M, MEM_HEADS, MEM_HEAD_DIM)
    vm = vm.reshape(B, M, MEM_HEADS, MEM_HEAD_DIM)
    lg = jnp.einsum('bshd,bmhd->bhsm', qm, km).astype(jnp.float32) * (MEM_HEAD_DIM ** -0.5)
    p = jax.nn.softmax(lg, axis=-1)
    o = jnp.einsum('bhsm,bmhd->bshd', p, vm)
    return o.reshape(qm.shape[0], qm.shape[1], MEM_W).astype(qm.dtype)


def conv_ffn(x, w_in, conv_w, conv_b, w_out):
    S = x.shape[1]
    a, b = jnp.split(x @ w_in, 2, axis=-1)
    a_pad = jnp.pad(a, ((0, 0), (CONV_W - 1, 0), (0, 0)))
    a = sum(conv_w[k] * a_pad[:, k:k + S] for k in range(CONV_W)) + conv_b
    return (jax.nn.gelu(a) * b) @ w_out


def setup_inputs(seed: int = 0) -> dict:
    key = jax.random.key(seed)
    keys = iter(jax.random.split(key, 32))

    def nrm(shape, scale):
        return jax.random.normal(next(keys), shape, jnp.float32) * scale

    L = CMP_LEN * HEAD_DIM
    return {
        'x': nrm((BATCH, SEQ, D_MODEL), 1.0),
        'mem': nrm((BATCH, N_MEM, D_MODEL), 1.0),
        'rel_bias': nrm((REL_BUCKETS, N_HEADS), 0.5),
        'a_w_in': nrm((N_A, D_MODEL, A_IN), D_MODEL ** -0.5),
        'a_cmp_pe_k': nrm((N_A, CMP_LEN, HEAD_DIM), 0.1),
        'a_cmp_w1_k': nrm((N_A, L, CMP_HIDDEN), L ** -0.5),
        'a_cmp_w2_k': nrm((N_A, CMP_HIDDEN, HEAD_DIM), CMP_HIDDEN ** -0.5),
        'a_cmp_pe_v': nrm((N_A, CMP_LEN, HEAD_DIM), 0.1),
        'a_cmp_w1_v': nrm((N_A, L, CMP_HIDDEN), L ** -0.5),
        'a_cmp_w2_v': nrm((N_A, CMP_HIDDEN, HEAD_DIM), CMP_HIDDEN ** -0.5),
        'a_w_mem_kv': nrm((N_A, D_MODEL, 2 * MEM_W), D_MODEL ** -0.5),
        'a_w_out': nrm((N_A, MIX_W, D_MODEL), BETA * MIX_W ** -0.5),
        'shared_w_kv': nrm((D_MODEL, 2 * MAIN_W), D_MODEL ** -0.5),
        'b_w_in': nrm((N_B, D_MODEL, B_IN), D_MODEL ** -0.5),
        'b_w_mem_kv': nrm((N_B, D_MODEL, 2 * MEM_W), D_MODEL ** -0.5),
        'b_w_out': nrm((N_B, MIX_W, D_MODEL), BETA * MIX_W ** -0.5),
        'ln1_g': 1.0 + nrm((DEPTH, D_MODEL), 0.02),
        'ln1_b': nrm((DEPTH, D_MODEL), 0.02),
        'ln2_g': 1.0 + nrm((DEPTH, D_MODEL), 0.02),
        'ln2_b': nrm((DEPTH, D_MODEL), 0.02),
        'ffn_w_in': nrm((DEPTH, D_MODEL, 2 * D_FF), D_MODEL ** -0.5),
        'ffn_conv_w': nrm((DEPTH, CONV_W, D_FF), CONV_W ** -0.5),
        'ffn_conv_b': nrm((DEPTH, D_FF), 0.02),
        'ffn_w_out': nrm((DEPTH, D_FF, D_MODEL), BETA * D_FF ** -0.5),
    }


def reference(x, mem, rel_bias, a_w_in, a_cmp_pe_k, a_cmp_w1_k, a_cmp_w2_k, a_cmp_pe_v, a_cmp_w1_v,
              a_cmp_w2_v, a_w_mem_kv, a_w_out, shared_w_kv, b_w_in, b_w_mem_kv, b_w_out,
              ln1_g, ln1_b, ln2_g, ln2_b, ffn_w_in, ffn_conv_w, ffn_conv_b, ffn_w_out):
    B, S, _ = x.shape

    def heads(z):
        return z.reshape(B, S, -1, HEAD_DIM)

    k_mean = k_blk = v_blk = None
    for layer in range(DEPTH):
        if layer < N_A:
            i = layer
            q, kc, vc, ks, vs, kw, vw, g, qm = split_cols(x @ a_w_in[i], A_IN_SIZES)
            k_cmp = nsa_compress(heads(kc), a_cmp_pe_k[i], a_cmp_w1_k[i], a_cmp_w2_k[i])
            v_cmp = nsa_compress(heads(vc), a_cmp_pe_v[i], a_cmp_w1_v[i], a_cmp_w2_v[i])
            gates = jax.nn.sigmoid(g).reshape(B, S, N_HEADS, 3)
            o_main = nsa_attention(heads(q), k_cmp, v_cmp, heads(ks), heads(vs), heads(kw), heads(vw),
                                   gates, rel_bias)
            w_mem_kv, w_out = a_w_mem_kv[i], a_w_out[i]
        else:
            if layer == N_A:
                k_mean, k_blk, v_blk = moba_shared_kv(x, shared_w_kv)
            i = layer - N_A
            q, qm = split_cols(x @ b_w_in[i], B_IN_SIZES)
            o_main = moba_attention(heads(q), k_mean, k_blk, v_blk, rel_bias)
            w_mem_kv, w_out = b_w_mem_kv[i], b_w_out[i]
        o_mem = memory_attention(qm.reshape(B, S, MEM_HEADS, MEM_HEAD_DIM), mem, w_mem_kv)
        mix = jnp.concatenate([o_main, o_mem], axis=-1) @ w_out
        x = layer_norm(ALPHA * x + mix, ln1_g[layer], ln1_b[layer])
        ffn = conv_ffn(x, ffn_w_in[layer], ffn_conv_w[layer], ffn_conv_b[layer], ffn_w_out[layer])
        x = layer_norm(ALPHA * x + ffn, ln2_g[layer], ln2_b[layer])
    return x
```

```python
import sys
import math
import numpy as np
from contextlib import ExitStack
import concourse.bass as bass
import concourse.mybir as mybir
from concourse.bass_utils import run_bass_kernel_spmd

F32 = mybir.dt.float32
BF16 = mybir.dt.bfloat16
AF = mybir.ActivationFunctionType
ALU = mybir.AluOpType

P = 128
D = 1024
S_LEN = 2048
NCORES = 8
NEG = -30000.0
ALPHA = 4.0 ** 0.25
LN_EPS = 1e-5
DFF = 2816
NFC = 22
TB_OFF = 384


class Sched:
    def __init__(self):
        self.ops = []
        self.lastw = {}
        self.readers = {}
        self.overlaps = {}

    def set_overlap(self, a, b):
        self.overlaps.setdefault(a, set()).add(b)
        self.overlaps.setdefault(b, set()).add(a)

    def _exp(self, k):
        o = self.overlaps.get(k)
        if o:
            return [k] + list(o)
        return [k]

    def add(self, eng, fns, reads=(), writes=(), dma_key=None, rg=None):
        if callable(fns):
            fns = [fns]
        deps = set()
        force = set()
        if rg is not None:
            for w0 in writes:
                lw = self.lastw.get(w0)
                if lw is not None and self.ops[lw].get('rg') is not None and not (self.ops[lw]['rg'] & rg):
                    force.add(lw)
            deps |= force
        for r0 in reads:
            for r in self._exp(r0):
                w = self.lastw.get(r)
                if w is not None:
                    deps.add(w)
            if r0.startswith('bank'):
                for ridx in self.readers.get(r0, ()):
                    if self.ops[ridx]['eng'] != eng:
                        deps.add(ridx)
        for w0 in writes:
            for w_ in self._exp(w0):
                w = self.lastw.get(w_)
                if w is not None:
                    deps.add(w)
                rl = self.readers.get(w_)
                if rl:
                    deps.update(rl)
        idx = len(self.ops)
        self.ops.append(dict(eng=eng, fns=fns, deps=deps, dma_key=dma_key, sig=None, rg=rg, force=force))
        for r in reads:
            self.readers.setdefault(r, []).append(idx)
        for w_ in writes:
            self.lastw[w_] = idx
            self.readers[w_] = []
        return idx

    def emit(self, nc, stack, final_wait_eng='sp'):
        ops = self.ops
        engh = dict(pe=nc.tensor, act=nc.scalar, dve=nc.vector, pool=nc.gpsimd, sp=nc.sync)
        needed = set()
        for op in ops:
            needed |= op['deps']
        EPOCH = 30000
        esems, ecount, dsems, dcount = {}, {}, {}, {}
        nsem = [0]

        def newsem(name):
            nsem[0] += 1
            return stack.enter_context(nc.semaphore(name))

        for idx, op in enumerate(ops):
            if op['dma_key'] is not None:
                k = op['dma_key']
                if k not in dsems:
                    dsems[k] = newsem("d%d_%s" % (nsem[0], k))
                    dcount[k] = 0
                dcount[k] += 16 * len(op['fns'])
                op['sig'] = (dsems[k], dcount[k])
            elif idx in needed:
                e = op['eng']
                if e not in esems or ecount[e] >= EPOCH:
                    esems[e] = newsem("e%s%d" % (e, nsem[0]))
                    ecount[e] = 0
                ecount[e] += 1
                op['sig'] = (esems[e], ecount[e])
        waited = {e: {} for e in engh}
        nwait = 0
        for idx, op in enumerate(ops):
            e = op['eng']
            h = engh[e]
            isdma = op['dma_key'] is not None
            best = {}
            for d in op['deps']:
                dop = ops[d]
                if dop['eng'] == 'pe' and e == 'pe' and dop['dma_key'] is None and not isdma and d not in op['force']:
                    continue
                sem, c = dop['sig']
                key = id(sem)
                if key not in best or best[key][1] < c:
                    best[key] = (sem, c)
            for key, (sem, c) in best.items():
                if waited[e].get(key, 0) >= c:
                    continue
                h.wait_ge(sem, c)
                nwait += 1
                waited[e][key] = c
            n = len(op['fns'])
            for i, fn in enumerate(op['fns']):
                ins = fn(h)
                if isdma:
                    ins.then_inc(op['sig'][0], 16)
                elif i == n - 1 and op['sig'] is not None:
                    ins.then_inc(op['sig'][0], 1)
        h = engh[final_wait_eng]
        for k, s in dsems.items():
            h.wait_ge(s, dcount[k])
        print("sched: ops=%d sems=%d waits=%d" % (len(ops), nsem[0], nwait), file=sys.stderr)


def _rel_bucket_np(n):
    n = np.maximum(n, 0)
    max_exact = 16
    nf = np.maximum(n, 1).astype(np.float32)
    large = max_exact + (np.log(nf / max_exact) / math.log(128 / max_exact) * (32 - max_exact)).astype(np.int32)
    large = np.minimum(large, 31)
    return np.where(n < max_exact, n, large)


def _host_tables(rel_bias):
    rb = np.asarray(rel_bias, np.float32)
    t = {}
    j = np.arange(128)[:, None]
    v = np.arange(1024)[None, :]
    d = v - TB_OFF - j
    bk = _rel_bucket_np(d)
    tbs = np.empty((12, 128, 1024), np.float32)
    for h in range(12):
        tbs[h] = np.where(d >= 0, rb[bk, h], np.float32(NEG))
    t['tbs'] = tbs
    npr = np.arange(48)[:, None]
    i = np.arange(512)[None, :]
    d = i - 16 * npr + 113
    bk = _rel_bucket_np(d)
    tbc = np.empty((12, 48, 512), np.float32)
    for h in range(12):
        tbc[h] = np.where(d >= 0, rb[bk, h], np.float32(NEG))
    t['tbc'] = tbc
    w = np.arange(896)[None, :]
    t['fm0'] = np.where(w - 384 - j >= 0, np.float32(NEG), np.float32(0.0)).astype(np.float32)
    n = np.arange(128)
    cbcol = np.zeros((128, 48), np.float32)
    for qc in range(4):
        n0 = 32 * qc - 9
        for h in range(12):
            col = np.where(n < n0, rb[31, h], np.where(n < n0 + 48, np.float32(0.0), np.float32(NEG)))
            cbcol[:, qc * 12 + h] = col
    t['cbcol'] = cbcol
    t['crow'] = np.broadcast_to(rb[31][None, :], (128, 12)).astype(np.float32).copy()
    tt = np.arange(16)[None, :, None]
    pp = np.arange(128)[:, None, None]
    tok = tt * 128 + pp
    cur = tok // 64
    bid = np.arange(32)[None, None, :]
    elig = bid <= cur
    forced = (bid == 0) | (bid == cur) | (bid == cur - 1)
    t['forced'] = np.where(elig, np.where(forced, 1.0e4, 0.0), -1.0e30).astype(np.float32)
    start = np.arange(127) * 16
    end = start + 32
    bs = np.arange(32) * 64
    ov = ((start[:, None] < bs[None, :] + 64) & (end[:, None] > bs[None, :])).astype(np.float32)
    ovp = np.zeros((128, 32), np.float32)
    ovp[:127] = ov
    t['ov'] = ovp
    c = np.arange(2048)[None, :]
    e32 = (c // 64 == np.arange(32)[:, None]).astype(np.float32)
    e96 = np.zeros((96, 2048), np.float32)
    e96[0:32] = e32
    e96[64:96] = e32
    t['e32'] = e96
    t['e8'] = (c // 256 == np.arange(8)[:, None]).astype(np.float32)
    cc = np.arange(264)[None, :]
    t['idw'] = (cc == np.arange(48)[:, None] + 128).astype(np.float32)
    cb = np.arange(8)[:, None]
    blk = np.arange(8)[None, :]
    mbneg = np.where(blk >= cb, np.float32(-1.0e30), np.float32(0.0)).reshape(1, 64)
    nvmk = np.where(blk < cb, np.float32(NEG), np.float32(0.0)).reshape(1, 64)
    t['mbneg'] = np.broadcast_to(mbneg, (128, 64)).astype(np.float32).copy()
    t['nvmk'] = np.broadcast_to(nvmk, (128, 64)).astype(np.float32).copy()
    return t


def _host_weights(inp):
    f = lambda a: np.ascontiguousarray(np.asarray(a, np.float32))
    w = {}
    awin = f(inp['a_w_in'])[0]
    seg = lambda a, b: awin[:, a:b]
    w['w0'] = f(np.concatenate([seg(0, 768), seg(1188, 1444), seg(768, 896), seg(896, 960), seg(896, 960),
                                seg(1024, 1088), seg(1024, 1088), seg(960, 1024), seg(1088, 1188)], axis=1))
    assert w['w0'].shape == (1024, 1572)
    w1k = f(inp['a_cmp_w1_k'])[0].reshape(32, 64, 256).transpose(1, 0, 2)
    w1v = f(inp['a_cmp_w1_v'])[0].reshape(32, 64, 256).transpose(1, 0, 2)
    w['w1cmp'] = f(np.concatenate([w1k, w1v], axis=0))
    w2k = f(inp['a_cmp_w2_k'])[0]
    w['w2k'] = f(np.concatenate([w2k, w2k], axis=1))
    w['w2v'] = f(inp['a_cmp_w2_v'])[0]
    w['pet'] = f(np.concatenate([f(inp['a_cmp_pe_k'])[0].T, f(inp['a_cmp_pe_v'])[0].T], axis=0))
    w['a_wmkv'] = f(inp['a_w_mem_kv'])[0]
    w['a_wout'] = f(inp['a_w_out'])[0]
    w['b_win'] = f(inp['b_w_in'])[0]
    w['skv'] = f(inp['shared_w_kv'])
    w['b_wmkv'] = f(inp['b_w_mem_kv'])[0]
    w['b_wout'] = f(inp['b_w_out'])[0]
    lnp = np.stack([np.stack([f(inp['ln1_g'])[l], f(inp['ln1_b'])[l], f(inp['ln2_g'])[l], f(inp['ln2_b'])[l]])
                    for l in range(2)])
    w['lnp'] = f(np.broadcast_to(lnp[:, :, None, :], (2, 4, 128, 1024)))
    fw = f(inp['ffn_w_in'])
    a = fw[:, :, :DFF].reshape(2, 8, 128, NFC, 128)
    b = fw[:, :, DFF:].reshape(2, 8, 128, NFC, 128)
    ab = np.concatenate([a, b], axis=4)
    w['fwin'] = f(ab.transpose(0, 3, 2, 1, 4))
    w['fwout'] = f(inp['ffn_w_out'])
    cw = f(inp['ffn_conv_w'])
    w['cw'] = f(cw.reshape(2, 3, NFC, 128).transpose(0, 3, 2, 1))
    w['cb'] = f(f(inp['ffn_conv_b']).reshape(2, NFC, 128).transpose(0, 2, 1))
    return w


INPUT_SHAPES = dict(
    w0=[1024, 1572], w1cmp=[128, 32, 256], w2k=[256, 128], w2v=[256, 64], pet=[128, 32],
    a_wmkv=[1024, 512], a_wout=[1024, 1024], b_win=[1024, 1024], skv=[1024, 1536],
    b_wmkv=[1024, 512], b_wout=[1024, 1024], lnp=[2, 4, 128, 1024],
    fwin=[2, NFC, 128, 8, 256], fwout=[2, DFF, 1024], cw=[2, 128, NFC, 3], cb=[2, 128, NFC],
    tbs=[12, 128, 1024], tbc=[12, 48, 512], fm0=[128, 896], cbcol=[128, 48], crow=[128, 12],
    forced=[128, 16, 32], ov=[128, 32], e32=[96, 2048], e8=[8, 2048], idw=[48, 264],
    mbneg=[128, 64], nvmk=[128, 64],
)


class KB:
    def __init__(self, nseq, debug=False, stop=None):
        self.nseq = nseq
        self.debug = debug
        self.stop = stop
        self.nc = bass.Bass("TRN2", target_bir_lowering=False)
        self.S = Sched()
        self.st = ExitStack()
        self.views = []
        self._bank_i = 0
        self._obank_i = 0
        self._bank6_i = 0
        self._bankT_i = 0
        self._ev = 0
        self._uid = 0
        self.jobs = []

    def sb(self, name, shape, dt):
        return self.st.enter_context(self.nc.sbuf_tensor(name, shape, dt))

    def din(self, name, shape, dt=F32):
        return self.nc.dram_tensor(name, shape, dt, kind="ExternalInput").ap()

    def view(self, key, off, shape, dt):
        n = 1
        for s_ in shape[1:]:
            n *= s_
        nel = n * (2 if dt == F32 else 1)
        ap = self.SCR[:, off:off + nel]
        if dt == F32:
            ap = ap.bitcast(F32)
        if len(shape) == 3:
            ap = ap.rearrange("p (a b) -> p a b", b=shape[2])
        elif len(shape) == 4:
            ap = ap.rearrange("p (a b c) -> p a b c", b=shape[2], c=shape[3])
        for (k2, o2, e2) in self.views:
            if off < e2 and o2 < off + nel:
                self.S.set_overlap(key, k2)
        self.views.append((key, off, off + nel))
        assert off + nel <= self.SCR_N, (key, off + nel)
        return ap

    def bank(self):
        i = self._bank_i % 4
        self._bank_i += 1
        return self.banks[i], "bank%d" % i

    def bank6(self):
        i = self._bank6_i % 6
        self._bank6_i += 1
        return self.banks[i], "bank%d" % i

    def obank(self):
        i = 4 + (self._obank_i % 2)
        self._obank_i += 1
        return self.banks[i], "bank%d" % i

    def bankT(self):
        i = self._bankT_i % len(self.banksT)
        self._bankT_i += 1
        return self.banksT[i], "bank%d" % (6 + i)

    def ring(self, name, n):
        c = getattr(self, "_r_" + name, 0)
        setattr(self, "_r_" + name, c + 1)
        return c % n

    def mm(self, out, lhsT, rhs, start, stop, reads, writes):
        p0 = lhsT.base_partition()
        kk = lhsT.shape[0]
        rg = frozenset(range(p0 // 32, (p0 + kk + 31) // 32))
        self.S.add('pe', lambda e: e.matmul(out, lhsT=lhsT, rhs=rhs, start=start, stop=stop, skip_group_check=True),
                   reads, writes, rg=rg)

    def tr(self, out, in_, reads, writes):
        ident = self.ident
        self.S.add('pe', lambda e: e.transpose(out, in_, ident[:]), list(reads) + ['ident'], writes)

    def dma(self, eng, out, in_, reads, writes, key):
        if len(out.shape) == 3 and out.shape[0] * out.shape[1] > 2048:
            fns = []
            for a in range(out.shape[1]):
                fns.append(lambda e, a=a: e.dma_start(out=out[:, a, :], in_=in_[:, a, :]))
            self.S.add(eng, fns, reads, writes, dma_key=key)
        else:
            self.S.add(eng, lambda e: e.dma_start(out=out, in_=in_), reads, writes, dma_key=key)

    def act(self, out, in_, func, reads, writes, bias=None, scale=None):
        kw = {}
        if bias is not None:
            kw['bias'] = bias
        if scale is not None:
            kw['scale'] = scale
        self.S.add('act', lambda e: e.activation(out=out, in_=in_, func=func, **kw), reads, writes)

    def copy(self, out, in_, reads, writes, eng=None):
        if eng is None:
            self._ev += 1
            eng = 'act' if self._ev % 2 else 'dve'
        if eng == 'act':
            self.S.add('act', lambda e: e.copy(out=out, in_=in_), reads, writes)
        else:
            self.S.add(eng, lambda e: e.tensor_copy(out=out, in_=in_), reads, writes)

    def scaled_copy(self, out, in_, mul, reads, writes):
        self._ev += 1
        if self._ev % 2:
            self.S.add('act', lambda e: e.mul(out=out, in_=in_, mul=mul), reads, writes)
        else:
            self.S.add('dve', lambda e: e.tensor_scalar_mul(out=out, in0=in_, scalar1=mul), reads, writes)

    def tt(self, out, in0, in1, op, reads, writes, eng='dve'):
        self.S.add(eng, lambda e: e.tensor_tensor(out=out, in0=in0, in1=in1, op=op), reads, writes)

    def dve(self, fn, reads, writes):
        self.S.add('dve', fn, reads, writes)

    def setup(self):
        nc, ns = self.nc, self.nseq
        d = {}
        d['x_tok'] = self.din('x_tok', [ns, S_LEN, D])
        d['xT'] = self.din('xT', [ns, D, S_LEN])
        d['memT'] = self.din('memT', [ns, D, 256])
        for k, shp in INPUT_SHAPES.items():
            d[k] = self.din(k, shp)
        self.d = d
        self.out = nc.dram_tensor('out', [ns, S_LEN, D], F32, kind="ExternalOutput").ap()
        skind = "ExternalOutput" if self.debug else "Internal"
        self.QS = nc.dram_tensor('QS', [8, P, S_LEN], BF16, kind=skind).ap()
        self.X1S = nc.dram_tensor('X1S', [S_LEN, D], F32, kind=skind).ap()
        self.X2S = nc.dram_tensor('X2S', [S_LEN, D], F32, kind=skind).ap()
        self.X2T = nc.dram_tensor('X2T', [D, S_LEN], BF16, kind=skind).ap()
        self.FWINB = nc.dram_tensor('FWINB', [2, NFC, P, 2048], BF16, kind="Internal").ap()
        self.FWOUTB = nc.dram_tensor('FWOUTB', [2, NFC, P, 1024], BF16, kind="Internal").ap()

        sb = self.sb
        self.ident = sb('ident', [P, P], BF16)
        self.identf = sb('identf', [P, P], F32)
        self.TBSr = [sb('TBS%d' % i, [P, 1024], BF16) for i in range(3)]
        self.TBCr = [sb('TBC%d' % i, [48, 512], BF16) for i in range(3)]
        self.FM0 = sb('FM0', [P, 896], BF16)
        self.E32 = sb('E32', [96, 2048], BF16)
        self.IDW = sb('IDW', [48, 264], BF16)
        self.CBCOL = sb('CBCOL', [P, 48], F32)
        self.CROW = sb('CROW', [P, 12], F32)
        self.FORCED = sb('FORCED', [P, 16, 32], F32)
        self.MBNEG = sb('MBNEG', [P, 64], F32)
        self.NVMK = sb('NVMK', [P, 64], F32)
        self.LNP = sb('LNP', [P, 4, 1024], F32)
        self.CW = sb('CW', [P, NFC, 3], F32)
        self.CBt = sb('CBt', [P, NFC], F32)
        self.EPS = sb('EPS', [P, 1], F32)
        self.PTK = sb('PTK', [P, 6, S_LEN], BF16)
        self.QCr = [sb('QC%d' % i, [P, 8, 512], BF16) for i in range(1)]
        self.VB = sb('VB', [P, 16, 12, 65], BF16)
        self.G = sb('G', [P, 16, 36], F32)
        self.KCT = sb('KCT', [P, 128], BF16)
        self.VCA = sb('VCA', [P, 97], BF16)
        self.GK = sb('GK', [P, 2, 128], BF16)
        self.GV = sb('GV', [P, 2, 128], BF16)
        self.W2K = sb('W2K', [P, 2, 128], BF16)
        self.W2V = sb('W2V', [P, 2, 64], BF16)
        self.PET = sb('PET', [P, 32], F32)
        self.KMT = sb('KMT', [P, 2, 256], BF16)
        self.VM = sb('VM', [P, 2, 4, 65], BF16)
        self.KMEANF = sb('KMEANF', [P, 6, 8], F32)
        self.KMEAN = sb('KMEAN', [P, 6, 8], BF16)
        self.WB = sb('WB', [P, 8, 1024], BF16)
        self.XTc = sb('XTc', [P, 8, 512], BF16)
        self.XRr = [sb('XR%d' % i, [P, 1024], F32) for i in range(4)]
        self.XBFr = [sb('XBF%d' % i, [P, 1024], BF16) for i in range(2)]
        self.HALO = sb('HALO', [P, NFC, 2], F32)
        self.RDr = [sb('RD%d' % i, [P, 4, 1], F32) for i in range(3)]
        self.WGr = [sb('WG%d' % i, [P, 4, 1], F32) for i in range(3)]
        self.TMPr = [sb('TMP%d' % i, [P, 4, 64], F32) for i in range(1)]
        self.IMPACC = sb('IMPACC', [P, 4, 32], F32)
        self.IMPT = sb('IMPT', [P, 4, 32], F32)
        self.SC = sb('SC', [P, 4, 32], F32)
        self.M8 = sb('M8', [P, 16], F32)
        self.WK = sb('WK', [P, 32], F32)
        self.PEN = sb('PEN', [P, 4, 96], BF16)
        self.PENT = sb('PENT', [96, 512], BF16)
        self.GM = sb('GM', [P, 12, 8], F32)
        self.M8M = sb('M8M', [P, 12, 8], F32)
        self.LT = sb('LT', [P, 12, 8], F32)
        self.PENM = sb('PENM', [P, 4, 12, 8], BF16)
        self.PENTM = sb('PENTM', [96, 512], BF16)
        self.BNSTr = [sb('BNST%d' % i, [P, 12], F32) for i in range(4)]
        self.MVr = [sb('MV%d' % i, [P, 2], F32) for i in range(4)]
        self.RSTDr = [sb('RSTD%d' % i, [P, 1], F32) for i in range(4)]
        self.NMR = sb('NMR', [P, 1], F32)
        self.SCR_N = 28192
        self.SCR = sb('SCR', [P, self.SCR_N], BF16)
        self.banks = [self.st.enter_context(nc.psum_tensor('pb%d' % i, [P, 512], F32)) for i in range(8)]
        self.banksT = [self.banks[6][:, :].bitcast(BF16), self.banks[7][:, :].bitcast(BF16)]

        V = self.view
        self.XCHr = [V('XCH0', 0, [P, 8, 512], BF16), V('XCH1', 4096, [P, 8, 512], BF16)]
        self.QST = V('QST', 8192, [P, 8, 512], BF16)
        self.WA = V('WA', 12288, [P, 8, 1572], BF16)
        self.WA1 = V('WA1', 12288, [P, 8, 1024], BF16)
        self.WA2 = V('WA2', 12288, [P, 8, 1536], BF16)
        self.W1 = V('W1', 0, [P, 32, 256], BF16)
        self.CA = V('CA', 8192, [P, 16, 128], BF16)
        self.CBv = V('CBv', 10240, [P, 16, 128], BF16)
        self.MEMT = V('MEMT', 12288, [P, 8, 256], BF16)
        self.WMKV = V('WMKV', 14336, [P, 8, 512], BF16)
        self.OACC = V('OACC', 0, [P, 4, 768], F32)
        self.OB = V('OB', 6144, [P, 4, 1024], BF16)
        self.OT = V('OT', 10240, [P, 8, 512], BF16)
        self.PRr = [V('PR%d' % i, 14336 + 512 * i, [P, 512], BF16) for i in range(4)]
        self.SAr = [V('SA%d' % i, 16384 + 1024 * i, [P, 512], F32) for i in range(3)]
        self.HT = V('HT', 0, [P, NFC, 512], BF16)
        self.ACr = [V('AC0', 11264, [P, 514], F32), V('AC1', 12296, [P, 514], F32), V('AC2', 25616, [P, 514], F32)]
        self.T1r = [V('T10', 13328, [P, 512], F32), V('T11', 14352, [P, 512], F32), V('T12', 26648, [P, 512], F32)]
        self.GBr = [V('GB0', 15376, [P, 512], BF16), V('GB1', 15888, [P, 512], BF16), V('GB2', 27672, [P, 512], BF16)]
        self.WINr = [V('WIN%d' % i, 16400 + 2048 * i, [P, 8, 256], BF16) for i in range(3)]
        self.WOUTr = [V('WOUT%d' % i, 22544 + 1024 * i, [P, 1024], BF16) for i in range(3)]
        subs = {'AC0': ['AC0h'], 'AC1': ['AC1h'], 'AC2': ['AC2h'],
                'OACC': ['OACC%d' % h for h in range(12)],
                'OB': ['OBm%d' % m for m in range(4)] + ['OBh%d' % h for h in range(12)] + ['OBmain'],
                'OT': ['OT%d' % j for j in range(4)]}
        for base, sl in subs.items():
            for sk in sl:
                for o in list(self.S.overlaps.get(base, ())):
                    self.S.set_overlap(sk, o)
        for h in range(12):
            self.S.set_overlap('OBmain', 'OBh%d' % h)

    def load_consts(self):
        S, d = self.S, self.d
        identf, ident = self.identf, self.ident
        S.add('pool', lambda e: e.memset(identf[:], 0.0), writes=['identf'])
        S.add('pool', lambda e: e.affine_select(out=identf[:], in_=identf[:], pattern=[[-1, P]],
                                                compare_op=ALU.not_equal, fill=1.0, base=0, channel_multiplier=1),
              reads=['identf'], writes=['identf'])
        S.add('dve', lambda e: e.tensor_copy(out=ident[:], in_=identf[:]), reads=['identf'], writes=['ident'])
        EPS, VB, VM, VCA = self.EPS, self.VB, self.VM, self.VCA
        S.add('dve', lambda e: e.memset(EPS[:], LN_EPS), writes=['EPS'])
        PEN_ = self.PEN
        S.add('dve', lambda e: e.memset(PEN_[:], 0.0), writes=['PEN'])
        S.add('dve', lambda e: e.memset(VB[:, :, :, 64:65], 1.0), writes=['VBones'])
        S.add('dve', lambda e: e.memset(VM[:, :, :, 64:65], 1.0), writes=['VMones'])
        S.add('dve', lambda e: e.memset(VCA[:, 64:65], 1.0), writes=['VCAones'])
        cast = [('FM0', self.FM0[:], d['fm0']), ('E32', self.E32[:], d['e32']),
                ('IDW', self.IDW[:], d['idw']), ('VCAov', self.VCA[:, 65:97], d['ov']),
                ('W2K', self.W2K[:], d['w2k'].rearrange("(c p) n -> p c n", p=P)),
                ('W2V', self.W2V[:], d['w2v'].rearrange("(c p) n -> p c n", p=P))]
        for k, o, i in cast:
            self.dma('pool', o, i, [], [k], 'c_' + k)
        plain = [('CBCOL', self.CBCOL[:], d['cbcol']), ('CROW', self.CROW[:], d['crow']),
                 ('FORCED', self.FORCED[:], d['forced']), ('MBNEG', self.MBNEG[:], d['mbneg']),
                 ('NVMK', self.NVMK[:], d['nvmk']), ('PET', self.PET[:], d['pet'])]
        for k, o, i in plain:
            self.dma('sp', o, i, [], [k], 'c_' + k)

    def precast_ffn_weights(self):
        d = self.d
        for l in range(2):
            for c in range(NFC):
                wi = self.ring('win', 3)
                WIN, wk = self.WINr[wi], 'WIN%d' % wi
                self.dma('pool', WIN.rearrange("p k n -> p (k n)"), d['fwin'][l, c].rearrange("p k n -> p (k n)"), [], [wk], 'f_' + wk)
                self.dma('sp', self.FWINB[l, c], WIN.rearrange("p k n -> p (k n)"), [wk], ['FWINB'], 'pc_' + wk)
                wo = self.ring('wout', 3)
                WO, wok = self.WOUTr[wo], 'WOUT%d' % wo
                self.dma('pool', WO, d['fwout'][l, c * 128:(c + 1) * 128, :], [], [wok], 'f_' + wok)
                self.dma('sp', self.FWOUTB[l, c], WO, [wok], ['FWOUTB'], 'pc_' + wok)

    def load_layer_consts(self, l):
        d = self.d
        self.dma('sp', self.LNP[:], d['lnp'][l].rearrange("f p n -> p f n"), [], ['LNP'], 'c_LNP')
        self.dma('sp', self.CW[:], d['cw'][l], [], ['CW'], 'c_CW')
        self.dma('sp', self.CBt[:], d['cb'][l], [], ['CBt'], 'c_CBt')
        HALO = self.HALO
        self.S.add('pool', lambda e: e.memset(HALO[:], 0.0), reads=[], writes=['HALO%d' % c for c in range(NFC)])

    def proj_pass(self, xsrc_fn, xeng, W, wkey, wsrc, ngroups_q, kgroups, tok_fn):
        self.dma('pool', W, wsrc, [], [wkey], 'w_' + wkey)
        lvl = {'proj1': 1, 'proj2': 2, 'proj3': 3, 'proj4': 4}.get(self.stop, 9)
        for qc in range(4):
            t0 = qc * 512
            ri = self.ring('xch', 2)
            XCH, xkey = self.XCHr[ri], 'XCH%d' % ri
            self.dma(xeng, XCH, xsrc_fn(qc), [], [xkey], 'x_' + xeng + xkey)
            if lvl < 2:
                continue
            for g in range(ngroups_q):
                B, bk = self.bank()
                for k in range(8):
                    self.mm(B[:], W[:, k, g * 128:(g + 1) * 128], XCH[:, k, :], k == 0, k == 7, [wkey, xkey], [bk])
                self.scaled_copy(self.QST[:, g, :], B[:], 0.125, [bk], ['QST'])
            if ngroups_q and lvl >= 3:
                self.dma('sp', self.QS[:, :, t0:t0 + 512].rearrange("g p t -> p g t"), self.QST[:, 0:8, :],
                         ['QST'], ['QS%d' % qc], 'qs_st')
            if lvl < 4:
                continue
            for (g, slot) in kgroups:
                B, bk = self.bank()
                for k in range(8):
                    self.mm(B[:], W[:, k, g * 128:(g + 1) * 128], XCH[:, k, :], k == 0, k == 7, [wkey, xkey], [bk])
                self.copy(self.PTK[:, slot, t0:t0 + 512], B[:], [bk], ['PTK%d_%d' % (slot, qc)])
            if tok_fn is not None and lvl >= 5:
                for j in range(4):
                    tok_fn(qc, j, XCH, xkey, W, wkey)

    def l0_tok(self, qc, j, XCH, xkey, W, wkey):
        tt = 4 * qc + j
        B, bk = self.bank()
        for k in range(8):
            self.mm(B[:, 0:164], XCH[:, k, j * 128:(j + 1) * 128], W[:, k, 1408:1572], k == 0, k == 7, [wkey, xkey], [bk])
        if self.stop == 'tok1':
            return
        self.copy(self.VB[:, tt, 0:2, 0:64], B[:, 0:128].rearrange("p (a b) -> p a b", b=64), [bk], ['VB%d' % tt])
        if self.stop == 'tok2':
            return
        G = self.G
        self.act(G[:, tt, :], B[:, 128:164], AF.Exp, [bk], ['G%d' % tt], scale=-1.0)
        self.dve(lambda e: e.tensor_scalar_add(out=G[:, tt, :], in0=G[:, tt, :], scalar1=1.0), ['G%d' % tt], ['G%d' % tt])
        self.dve(lambda e: e.reciprocal(out=G[:, tt, :], in_=G[:, tt, :]), ['G%d' % tt], ['G%d' % tt])

    def l1_tok(self, qc, j, XCH, xkey, W, wkey):
        tt = 4 * qc + j
        B, bk = self.bank()
        B2, bk2 = self.bank()
        for k in range(8):
            self.mm(B[:], XCH[:, k, j * 128:(j + 1) * 128], W[:, k, 768:1280], k == 0, k == 7, [wkey, xkey], [bk])
        for k in range(8):
            self.mm(B2[:, 0:256], XCH[:, k, j * 128:(j + 1) * 128], W[:, k, 1280:1536], k == 0, k == 7, [wkey, xkey], [bk2])
        self.copy(self.VB[:, tt, 0:8, 0:64], B[:].rearrange("p (a b) -> p a b", b=64), [bk], ['VB%d' % tt])
        self.copy(self.VB[:, tt, 8:12, 0:64], B2[:, 0:256].rearrange("p (a b) -> p a b", b=64), [bk2], ['VB%d' % tt])

    def compress(self):
        d = self.d
        W1, CA, CBv, PET, PTK = self.W1, self.CA, self.CBv, self.PET, self.PTK
        self.dma('pool', W1, d['w1cmp'], [], ['W1'], 'w_W1')
        src = PTK[:, 0, :].rearrange("p (i l) -> p l i", l=16)
        kreads = ['PTK0_%d' % q for q in range(4)]
        self.tt(CA, src, PET[:, 0:16].unsqueeze(2).to_broadcast([P, 16, 128]), ALU.add, kreads + ['PET'], ['CA'])
        self.tt(CBv, src, PET[:, 16:32].unsqueeze(2).to_broadcast([P, 16, 128]), ALU.add, kreads + ['PET'], ['CBv'])
        for (pb, GX, gkey) in ((0, self.GK, 'GK'), (64, self.GV, 'GV')):
            for hc in range(2):
                B, bk = self.bank()
                for l in range(32):
                    rhs = CA[pb:pb + 64, l, 0:127] if l < 16 else CBv[pb:pb + 64, l - 16, 1:128]
                    self.mm(B[:, 0:127], W1[pb:pb + 64, l, hc * 128:(hc + 1) * 128], rhs, l == 0, l == 31,
                            ['W1', 'CA', 'CBv'], [bk])
                self.act(GX[:, hc, 0:127], B[:, 0:127], AF.Gelu_apprx_tanh, [bk], [gkey])
        B, bk = self.bank()
        for c in range(2):
            self.mm(B[:, 0:127], self.W2K[:, c, :], self.GK[:, c, 0:127], c == 0, c == 1, ['W2K', 'GK'], [bk])
        self.copy(self.KCT[:, 0:127], B[:, 0:127], [bk], ['KCT'])
        B, bk = self.bank()
        for c in range(2):
            self.mm(B[0:127, 0:64], self.GV[:, c, 0:127], self.W2V[:, c, :], c == 0, c == 1, ['W2V', 'GV'], [bk])
        self.copy(self.VCA[0:127, 0:64], B[0:127, 0:64], [bk], ['VCAv'])

    def mem_kv(self, s, wname):
        d = self.d
        MEMT, WMKV = self.MEMT, self.WMKV
        self.dma('pool', MEMT, d['memT'][s].rearrange("(k p) m -> p k m", p=P), [], ['MEMT'], 'w_MEMT')
        self.dma('pool', WMKV, d[wname].rearrange("(k p) n -> p k n", p=P), [], ['WMKV'], 'w_WMKV')
        for g in range(2):
            B, bk = self.bank()
            for k in range(8):
                self.mm(B[:, 0:256], WMKV[:, k, g * 128:(g + 1) * 128], MEMT[:, k, :], k == 0, k == 7, ['MEMT', 'WMKV'], [bk])
            self.copy(self.KMT[:, g, :], B[:, 0:256], [bk], ['KMT'])
        for mt in range(2):
            B, bk = self.bank()
            for k in range(8):
                self.mm(B[:, 0:256], MEMT[:, k, mt * 128:(mt + 1) * 128], WMKV[:, k, 256:512], k == 0, k == 7,
                        ['MEMT', 'WMKV'], [bk])
            self.copy(self.VM[:, mt, :, 0:64], B[:, 0:256].rearrange("p (a b) -> p a b", b=64), [bk], ['VMv'])

    def attn(self, q_ap, q_reads, ktiles, ncols, post, pre=None):
        first, last = {}, {}
        for i, kt in enumerate(ktiles):
            for j in kt['js']:
                first.setdefault(j, i)
                last[j] = i
        assert len(first) == 4
        self.jobs.append(dict(q=q_ap, qr=list(q_reads), kts=ktiles, ncols=ncols, post=post, pre=pre, last=last,
                              O=None, pv_started=False, pre_done=False))

    def flush_attn(self, D=3):
        jobs = self.jobs
        self.jobs = []
        items = []
        for jn, job in enumerate(jobs):
            for i in range(len(job['kts'])):
                items.append((jn, i))
        N = len(items)
        staged = {}
        for idx in range(N + D):
            if idx < N:
                jn, i = items[idx]
                job = jobs[jn]
                if i == 0:
                    for jj in range(jn, min(jn + 3, len(jobs))):
                        if jobs[jj]['pre'] is not None and not jobs[jj]['pre_done']:
                            jobs[jj]['pre']()
                            jobs[jj]['pre_done'] = True
                    Ob, okey = self.obank()
                    nco = job['ncols']
                    job['O'] = (Ob[:, 0:4 * nco].rearrange("p (j c) -> p j c", c=nco), okey)
                kt = job['kts'][i]
                Sb, skey = self.bank()
                nk = kt['nk']
                js = kt['js']
                assert js == list(range(js[0], js[-1] + 1))
                qlo, qhi = js[0] * 128, (js[-1] + 1) * 128
                dve_bias = [x for x in kt['extras'] if x[0] is self.ident[:] or getattr(x[0], '_is_ident', False)]
                dve_bias = [x for x in kt['extras'] if x[2] and x[2][0] == 'ident' and x[2][1].startswith('TBS')]
                pe_extras = [x for x in kt['extras'] if x not in dve_bias]
                mms = [(kt['k'][0], job['q'], list(kt['k'][1]) + job['qr'])] + pe_extras
                for m, (l, r, rd) in enumerate(mms):
                    self.mm(Sb[0:nk, qlo:qhi], l, r[:, qlo:qhi], m == 0, m == len(mms) - 1, rd, [skey])
                pi = self.ring('pr', 4)
                Pb, pkey = self.PRr[pi], 'PR%d' % pi
                src_ap, src_key = Sb, skey
                if dve_bias:
                    si2 = self.ring('sadd', 3)
                    SA, sakey = self.SAr[si2], 'SA%d' % si2
                    (_, tb, trd) = dve_bias[0]
                    self.tt(SA[0:nk, qlo:qhi], Sb[0:nk, qlo:qhi], tb[:, qlo:qhi], ALU.add, [skey, trd[1]], [sakey])
                    src_ap, src_key = SA, sakey
                Sb, skey = src_ap, src_key
                if kt['bias'] is not None:
                    self.act(Pb[0:nk, qlo:qhi], Sb[0:nk, qlo:qhi], AF.Exp, [skey] + list(kt['bias'][1]), [pkey],
                             bias=kt['bias'][0])
                else:
                    self.act(Pb[0:nk, qlo:qhi], Sb[0:nk, qlo:qhi], AF.Exp, [skey], [pkey])
                staged[idx] = (Pb, pkey)
            k = idx - D
            if k >= 0:
                jn, i = items[k]
                job = jobs[jn]
                kt = job['kts'][i]
                nk = kt['nk']
                Pb, pkey = staged.pop(k)
                Ov, okey = job['O']
                for j in kt['js']:
                    self.mm(Ov[:, j, :], Pb[0:nk, j * 128:(j + 1) * 128], kt['v'][0], not job['pv_started'],
                            job['last'][j] == i, [pkey] + list(kt['v'][1]), [okey])
                    job['pv_started'] = True
                if i == len(job['kts']) - 1:
                    job['post'](Ov, okey)

    def norm(self, Ov, okey, gate, dst, dkey, add, imp=None):
        ri = self.ring('rd', 3)
        RD, rkey = self.RDr[ri], 'RD%d' % ri
        self.dve(lambda e: e.tensor_scalar_max(out=RD[:], in0=Ov[:, :, 64:65], scalar1=1e-30), [okey], [rkey])
        self.dve(lambda e: e.reciprocal(out=RD[:], in_=RD[:]), [rkey], [rkey])
        if imp is not None:
            first = imp
            if first:
                self.tt(self.IMPACC[:], Ov[:, :, 65:97], RD[:].to_broadcast([P, 4, 32]), ALU.mult, [okey, rkey], ['IMPACC'])
            else:
                self.tt(self.IMPT[:], Ov[:, :, 65:97], RD[:].to_broadcast([P, 4, 32]), ALU.mult, [okey, rkey], ['IMPT'])
                self.tt(self.IMPACC[:], self.IMPACC[:], self.IMPT[:], ALU.add, ['IMPACC', 'IMPT'], ['IMPACC'])
        W, wkey = RD, rkey
        if gate is not None:
            wi = self.ring('wg', 3)
            W, wkey = self.WGr[wi], 'WG%d' % wi
            self.tt(W[:], RD[:], gate[0], ALU.mult, [rkey] + list(gate[1]), [wkey])
        if not add:
            self.tt(dst, Ov[:, :, 0:64], W[:].to_broadcast([P, 4, 64]), ALU.mult, [okey, wkey], [dkey])
        else:
            ti = self.ring('tmp', 1)
            T, tkey = self.TMPr[ti], 'TMP%d' % ti
            self.tt(T[:], Ov[:, :, 0:64], W[:].to_broadcast([P, 4, 64]), ALU.mult, [okey, wkey], [tkey])
            self.tt(dst, dst, T[:], ALU.add, [dkey, tkey], [dkey])

    def load_q(self, qc):
        ri = self.ring('qc', 1)
        QC, qkey = self.QCr[ri], 'QC%d' % ri
        t0 = qc * 512
        self.dma('sp', QC[:], self.QS[:, :, t0:t0 + 512].rearrange("g p t -> p g t"), ['QS%d' % qc], [qkey], 'q_' + qkey)
        return QC, qkey

    def get_q(self, qc):
        if getattr(self, '_qpre', None) is not None and self._qpre[0] == qc:
            r = self._qpre[1]
            self._qpre = None
            return r
        return self.load_q(qc)

    def prefetch_q(self, qc):
        self._qpre = (qc, self.load_q(qc))

    def mem_attn(self, QC, qkey):
        for mh in range(4):
            pb = (mh % 2) * 64
            g = 6 + mh // 2
            kts = []
            for mt in range(2):
                kts.append(dict(k=(self.KMT[pb:pb + 64, mh // 2, mt * 128:(mt + 1) * 128], ['KMT']), extras=[], bias=None,
                                nk=128, v=(self.VM[:, mt, mh, :], ['VMv', 'VMones']), js=[0, 1, 2, 3]))
            def post(Ov, okey, mh=mh):
                self.norm(Ov, okey, None, self.OB[:, :, 768 + mh * 64:768 + (mh + 1) * 64], 'OBm%d' % mh, False)
            self.attn(QC[pb:pb + 64, g, :], [qkey], kts, 65, post)

    def nsa_chunk(self, qc):
        d = self.d
        t0 = qc * 512
        QC, qkey = self.get_q(qc)
        gkeys = ['G%d' % (4 * qc + j) for j in range(4)]
        need_sel = qc >= 2
        n0 = 32 * qc - 9
        for h in range(12):
            pb, g = (h % 2) * 64, h // 2
            ci = self.ring('tbc', 3)
            TBC, ckey = self.TBCr[ci], 'TBC%d' % ci
            def pre(TBC=TBC, ckey=ckey, h=h):
                self.dma('pool', TBC[:], d['tbc'][h], [], [ckey], 't_' + ckey)
            kt = dict(k=(self.KCT[pb:pb + 64, 0:127], ['KCT']),
                      extras=[(self.IDW[0:48, 128 - n0:128 - n0 + 127], TBC[0:48, :], ['IDW', ckey])],
                      bias=(self.CBCOL[0:127, qc * 12 + h:qc * 12 + h + 1], ['CBCOL']), nk=127,
                      v=(self.VCA[0:127, 0:97], ['VCAv', 'VCAones', 'VCAov']), js=[0, 1, 2, 3])
            def post(Ov, okey, h=h):
                gate = (self.G[:, 4 * qc:4 * qc + 4, 3 * h:3 * h + 1], gkeys)
                self.norm(Ov, okey, gate, self.OACC[:, :, h * 64:(h + 1) * 64], 'OACC%d' % h, False,
                          imp=((h == 0) if need_sel else None))
            self.attn(QC[pb:pb + 64, g, :], [qkey], [kt], 97, post, pre)
        self.flush_attn()
        if need_sel:
            SC, M8, WK, PEN, PENT = self.SC, self.M8, self.WK, self.PEN, self.PENT
            self.tt(SC[:], self.IMPACC[:], self.FORCED[:, 4 * qc:4 * qc + 4, :], ALU.add, ['IMPACC', 'FORCED'], ['SC'])
            for j in range(4):
                def f1(e, j=j):
                    return e.max(out=M8[:, 0:8], in_=SC[:, j, :])
                def f2(e, j=j):
                    return e.match_replace(out=WK[:], in_to_replace=M8[:, 0:8], in_values=SC[:, j, :], imm_value=-1e30)
                def f3(e, j=j):
                    return e.max(out=M8[:, 8:16], in_=WK[:])
                def f4(e, j=j):
                    return e.tensor_scalar(out=PEN[:, j, 0:32], in0=SC[:, j, :], scalar1=M8[:, 15:16], scalar2=NEG,
                                           op0=ALU.is_lt, op1=ALU.mult)
                def f5(e, j=j):
                    return e.tensor_scalar(out=PEN[:, j, 64:96], in0=SC[:, j, :], scalar1=M8[:, 15:16], scalar2=NEG,
                                           op0=ALU.is_lt, op1=ALU.mult)
                self.dve(f1, ['SC'], ['M8a'])
                self.dve(f2, ['SC', 'M8a'], ['WK'])
                self.dve(f3, ['WK'], ['M8b'])
                self.dve(f4, ['SC', 'M8b'], ['PEN'])
                self.dve(f5, ['SC', 'M8b'], ['PEN'])
            BT, btk = self.bankT()
            for j in range(4):
                self.tr(BT[0:96, j * 128:(j + 1) * 128], PEN[:, j, :], ['PEN'], [btk])
            self.copy(PENT[:], BT[0:96, 0:512], [btk], ['PENT'])
        for h in range(12):
            pb, g = (h % 2) * 64, h // 2
            si = self.ring('tbs', 3)
            TBS, tkey = self.TBSr[si], 'TBS%d' % si
            def pre(TBS=TBS, tkey=tkey, h=h):
                self.dma('pool', TBS[:], d['tbs'][h], [], [tkey], 't_' + tkey)
            crow = (self.CROW[:, h:h + 1], ['CROW'])
            q_ap = QC[pb:pb + 64, g, :]
            kts = []
            for kt_ in range(0, 4 * qc + 4):
                rel = t0 - 128 * kt_
                ex = []
                if rel <= 128:
                    ex.append((self.ident[:], TBS[:, rel + TB_OFF:rel + TB_OFF + 512], ['ident', tkey]))
                if need_sel:
                    ex.append((self.E32[pb:pb + 32, kt_ * 128:(kt_ + 1) * 128], self.PENT[pb:pb + 32, :], ['E32', 'PENT']))
                kts.append(dict(k=(self.PTK[pb:pb + 64, 1, kt_ * 128:(kt_ + 1) * 128], ['PTK1_%d' % (kt_ // 4)]), extras=ex,
                                bias=(crow if rel >= 256 else None), nk=128,
                                v=(self.VB[:, kt_, 0, :], ['VB%d' % kt_, 'VBones']),
                                js=[j for j in range(4) if 4 * qc + j >= kt_]))
            def post(Ov, okey, h=h):
                gate = (self.G[:, 4 * qc:4 * qc + 4, 3 * h + 1:3 * h + 2], gkeys)
                self.norm(Ov, okey, gate, self.OACC[:, :, h * 64:(h + 1) * 64], 'OACC%d' % h, True)
            self.attn(q_ap, [qkey], kts, 65, post, pre)
            kts = []
            for kt_ in range(max(0, 4 * qc - 4), 4 * qc + 4):
                rel = t0 - 128 * kt_
                ex = []
                if rel <= 128:
                    ex.append((self.ident[:], TBS[:, rel + TB_OFF:rel + TB_OFF + 512], ['ident', tkey]))
                if rel >= 128:
                    ex.append((self.ident[:], self.FM0[:, rel - 128:rel - 128 + 512], ['ident', 'FM0']))
                js = [j for j in range(4) if 4 * qc + j >= kt_ and rel + 128 * j - 127 < 512]
                kts.append(dict(k=(self.PTK[pb:pb + 64, 2, kt_ * 128:(kt_ + 1) * 128], ['PTK2_%d' % (kt_ // 4)]), extras=ex,
                                bias=(crow if rel >= 256 else None), nk=128,
                                v=(self.VB[:, kt_, 1, :], ['VB%d' % kt_, 'VBones']), js=js))
            def post(Ov, okey, h=h):
                gate = (self.G[:, 4 * qc:4 * qc + 4, 3 * h + 2:3 * h + 3], gkeys)
                self.norm(Ov, okey, gate, self.OACC[:, :, h * 64:(h + 1) * 64], 'OACC%d' % h, True)
            self.attn(q_ap, [qkey], kts, 65, post)
        self.mem_attn(QC, qkey)
        self.flush_attn()
        self.copy(self.OB[:, :, 0:768], self.OACC[:], ['OACC%d' % h for h in range(12)], ['OBmain'], eng='act')

    def moba_chunk(self, qc):
        d = self.d
        t0 = qc * 512
        QC, qkey = self.get_q(qc)
        GM, M8M, LT, PENM, PENTM = self.GM, self.M8M, self.LT, self.PENM, self.PENTM
        for j in range(4):
            tt = 4 * qc + j
            cb = tt // 2
            B, bk = self.bank()
            Gv = B[:, 0:96].rearrange("p (h b) -> p h b", b=8)
            for h in list(range(0, 12, 2)) + list(range(1, 12, 2)):
                pb, g = (h % 2) * 64, h // 2
                self.mm(Gv[:, h, :], QC[pb:pb + 64, g, j * 128:(j + 1) * 128], self.KMEAN[pb:pb + 64, g, :], True, True,
                        [qkey, 'KMEAN'], [bk])
            self.tt(GM[:], Gv, self.MBNEG[:, cb * 8:(cb + 1) * 8].unsqueeze(1).to_broadcast([P, 12, 8]), ALU.add,
                    [bk, 'MBNEG'], ['GM'])
            for h in range(12):
                def fm(e, h=h):
                    return e.max(out=M8M[:, h, :], in_=GM[:, h, :])
                self.dve(fm, ['GM'], ['M8M'])
            self.tt(LT[:], GM[:], M8M[:, :, 2:3].to_broadcast([P, 12, 8]), ALU.is_lt, ['GM', 'M8M'], ['LT'])
            self.tt(PENM[:, j, :, :], LT[:], self.NVMK[:, cb * 8:(cb + 1) * 8].unsqueeze(1).to_broadcast([P, 12, 8]),
                    ALU.mult, ['LT', 'NVMK'], ['PENM'])
        BT, btk = self.bankT()
        for j in range(4):
            self.tr(BT[0:96, j * 128:(j + 1) * 128], PENM[:, j, :, :].rearrange("p h b -> p (h b)"), ['PENM'], [btk])
        self.copy(PENTM[:, :], BT[0:96, 0:512], [btk], ['PENTM'])
        for h in range(12):
            pb, g = (h % 2) * 64, h // 2
            si = self.ring('tbs', 3)
            TBS, tkey = self.TBSr[si], 'TBS%d' % si
            def pre(TBS=TBS, tkey=tkey, h=h):
                self.dma('pool', TBS[:], d['tbs'][h], [], [tkey], 't_' + tkey)
            crow = (self.CROW[:, h:h + 1], ['CROW'])
            kts = []
            for kt_ in range(0, 4 * qc + 4):
                rel = t0 - 128 * kt_
                b = kt_ // 2
                ex = []
                if rel <= 128:
                    ex.append((self.ident[:], TBS[:, rel + TB_OFF:rel + TB_OFF + 512], ['ident', tkey]))
                if b < 2 * qc + 1:
                    ex.append((self.ident[0:96, 8 * h + b:8 * h + b + 1].to_broadcast([96, 128]), PENTM[:, :],
                               ['ident', 'PENTM']))
                kts.append(dict(k=(self.PTK[pb:pb + 64, g, kt_ * 128:(kt_ + 1) * 128], ['PTK%d_%d' % (g, kt_ // 4)]), extras=ex,
                                bias=(crow if rel >= 256 else None), nk=128,
                                v=(self.VB[:, kt_, h, :], ['VB%d' % kt_, 'VBones']),
                                js=[j for j in range(4) if 4 * qc + j >= kt_]))
            def post(Ov, okey, h=h):
                self.norm(Ov, okey, None, self.OB[:, :, h * 64:(h + 1) * 64], 'OBh%d' % h, False)
            self.attn(QC[pb:pb + 64, g, :], [qkey], kts, 65, post, pre)
        self.mem_attn(QC, qkey)
        self.flush_attn()

    def ln_ops(self, XR, xk, halves, lidx):
        li = self.ring('lnsm', 4)
        BNST, MV, RSTD = self.BNSTr[li], self.MVr[li], self.RSTDr[li]
        kb, km, kr = 'BNST%d' % li, 'MV%d' % li, 'RSTD%d' % li
        EPS, LNP = self.EPS, self.LNP
        ops = []
        for hf, (B, bk) in enumerate(halves):
            def f(e, hf=hf, B=B):
                return e.scalar_tensor_tensor(out=XR[:, hf * 512:(hf + 1) * 512], in0=XR[:, hf * 512:(hf + 1) * 512],
                                              scalar=ALPHA, in1=B, op0=ALU.mult, op1=ALU.add)
            ops.append(lambda f=f, bk=bk: self.dve(f, [xk, bk], [xk]))
        ops.append(lambda: self.dve(lambda e: e.bn_stats(out=BNST[:, 0:6], in_=XR[:, 0:512]), [xk], [kb]))
        ops.append(lambda: self.dve(lambda e: e.bn_stats(out=BNST[:, 6:12], in_=XR[:, 512:1024]), [xk], [kb]))
        ops.append(lambda: self.dve(lambda e: e.bn_aggr(out=MV[:], in_=BNST[:]), [kb], [km]))
        ops.append(lambda: self.act(RSTD[:], MV[:, 1:2], AF.Sqrt, [km, 'EPS'], [kr], bias=EPS[:]))
        ops.append(lambda: self.dve(lambda e: e.reciprocal(out=RSTD[:], in_=RSTD[:]), [kr], [kr]))
        ops.append(lambda: self.dve(lambda e: e.tensor_scalar(out=XR[:], in0=XR[:], scalar1=MV[:, 0:1], scalar2=RSTD[:],
                                                              op0=ALU.subtract, op1=ALU.mult), [xk, km, kr], [xk]))
        ops.append(lambda: self.tt(XR[:], XR[:], LNP[:, lidx, :], ALU.mult, [xk, 'LNP'], [xk]))
        ops.append(lambda: self.tt(XR[:], XR[:], LNP[:, lidx + 1, :], ALU.add, [xk, 'LNP'], [xk]))
        return ops

    @staticmethod
    def interleave(lists):
        n = max(len(l) for l in lists)
        for k in range(n):
            for l in lists:
                if k < len(l):
                    l[k]()

    def to_fm(self, XR, xk, dst, dkey, j):
        bi = self.ring('xbf', 2)
        XBF, bfk = self.XBFr[bi], 'XBF%d' % bi
        self.copy(XBF[:], XR[:], [xk], [bfk], eng='act')
        BT, btk = self.bankT()
        for f in range(8):
            self.tr(BT[:, f * 128:(f + 1) * 128], XBF[:, f * 128:(f + 1) * 128], [bfk], [btk])
        self.copy(dst[:, :, j * 128:(j + 1) * 128], BT[:].rearrange("p (a b) -> p a b", b=128), [btk], [dkey])

    def mixer_tail(self, s, l, qc):
        t0 = qc * 512
        obk = ['OBmain'] + ['OBm%d' % m for m in range(4)] if l == 0 else \
              ['OBh%d' % h for h in range(12)] + ['OBm%d' % m for m in range(4)]
        for j in range(4):
            BT, btk = self.bankT()
            for f in range(8):
                self.tr(BT[:, f * 128:(f + 1) * 128], self.OB[:, j, f * 128:(f + 1) * 128], obk, [btk])
            self.copy(self.OT[:, :, j * 128:(j + 1) * 128], BT[:].rearrange("p (a b) -> p a b", b=128), [btk], ['OT%d' % j])
        allb = [(self.banks[b], 'bank%d' % b) for b in range(8)]
        for pair in ((0, 1, 2, 3),):
            chains, info = [], []
            for j in pair:
                r0 = t0 + j * 128
                xi = self.ring('xr', 4)
                XR, xk = self.XRr[xi], 'XR%d' % xi
                if l == 0:
                    self.dma('sp', XR[:], self.d['x_tok'][s, r0:r0 + 128, :], [], [xk], 'r_' + xk)
                else:
                    self.dma('sp', XR[:], self.X2S[r0:r0 + 128, :], ['X2S%d' % (r0 // 128)], [xk], 'r_' + xk)
                halves = []
                for hf in range(2):
                    B, bk = allb[j * 2 + hf]
                    for f in range(8):
                        self.mm(B[:], self.OT[:, f, j * 128:(j + 1) * 128], self.WB[:, f, hf * 512:(hf + 1) * 512], f == 0,
                                f == 7, ['OT%d' % j, 'WB'], [bk])
                    halves.append((B[:], bk))
                chains.append(self.ln_ops(XR, xk, halves, 0))
                info.append((j, r0, xi, XR, xk))
            self.interleave(chains)
            for (j, r0, xi, XR, xk) in info:
                self.dma('sp', self.X1S[r0:r0 + 128, :], XR[:], [xk], ['X1S%d' % (r0 // 128)], 'st_x1_%d' % xi)
                self.to_fm(XR, xk, self.XTc, 'XTc', j)

    def ffn_chunk(self, s, l, qc):
        d = self.d
        t0 = qc * 512
        HT, XTc, CW, CBt, HALO = self.HT, self.XTc, self.CW, self.CBt, self.HALO
        for c in range(NFC):
            wi = self.ring('win', 3)
            WIN, wk = self.WINr[wi], 'WIN%d' % wi
            self.dma('sp', WIN.rearrange("p k n -> p (k n)"), self.FWINB[l, c], ['FWINB'], [wk], 'g_' + wk)
            PA, pak = self.bank6()
            PB, pbk = self.bank6()
            for k in range(8):
                self.mm(PA[:], WIN[:, k, 0:128], XTc[:, k, :], k == 0, k == 7, [wk, 'XTc'], [pak])
            for k in range(8):
                self.mm(PB[:], WIN[:, k, 128:256], XTc[:, k, :], k == 0, k == 7, [wk, 'XTc'], [pbk])
            ai = self.ring('ac', 3)
            AC, ak = self.ACr[ai], 'AC%d' % ai
            T1, tk = self.T1r[ai], 'T1%d' % ai
            GB, gk = self.GBr[ai], 'GB%d' % ai
            hk = 'HALO%d' % c
            self.copy(AC[:, 0:2], HALO[:, c, :], [hk], [ak + 'h'], eng='pool')
            self.copy(AC[:, 2:514], PA[:], [pak], [ak], eng='act')
            self.act(T1[:], PA[:], AF.Identity, [pak, 'CW', 'CBt'], [tk], bias=CBt[:, c:c + 1], scale=CW[:, c, 2:3])
            def f1(e, AC=AC, T1=T1, c=c):
                return e.scalar_tensor_tensor(out=T1[:], in0=AC[:, 1:513], scalar=CW[:, c, 1:2], in1=T1[:],
                                              op0=ALU.mult, op1=ALU.add)
            def f0(e, AC=AC, T1=T1, c=c):
                return e.scalar_tensor_tensor(out=T1[:], in0=AC[:, 0:512], scalar=CW[:, c, 0:1], in1=T1[:],
                                              op0=ALU.mult, op1=ALU.add)
            self.dve(f1, [ak, ak + 'h', tk, 'CW'], [tk])
            self.dve(f0, [ak, ak + 'h', tk, 'CW'], [tk])
            self.copy(HALO[:, c, :], AC[:, 512:514], [ak], [hk], eng='pool')
            self.act(GB[:], T1[:], AF.Gelu_apprx_tanh, [tk], [gk])
            self.tt(HT[:, c, :], GB[:], PB[:], ALU.mult, [gk, pbk], ['HT'])
        order = [6, 7, 0, 1, 2, 3, 4, 5]
        accs = [(self.banks[b], 'bank%d' % b) for b in order]
        for c in range(NFC):
            wi = self.ring('wout', 3)
            WO, wk = self.WOUTr[wi], 'WOUT%d' % wi
            self.dma('sp', WO, self.FWOUTB[l, c], ['FWOUTB'], [wk], 'g_' + wk)
            for j in range(4):
                for hf in range(2):
                    B, bk = accs[j * 2 + hf]
                    self.mm(B[:], HT[:, c, j * 128:(j + 1) * 128], WO[:, hf * 512:(hf + 1) * 512], c == 0, c == NFC - 1,
                            ['HT', wk], [bk])
        for pair in ((0, 1, 2, 3),):
            chains, info = [], []
            for j in pair:
                r0 = t0 + j * 128
                xi = self.ring('xr', 4)
                XR, xk = self.XRr[xi], 'XR%d' % xi
                self.dma('sp', XR[:], self.X1S[r0:r0 + 128, :], ['X1S%d' % (r0 // 128)], [xk], 'r_' + xk)
                halves = [(accs[j * 2 + hf][0][:], accs[j * 2 + hf][1]) for hf in range(2)]
                chains.append(self.ln_ops(XR, xk, halves, 2))
                info.append((j, r0, xi, XR, xk))
            self.interleave(chains)
            for (j, r0, xi, XR, xk) in info:
                if l == 0:
                    self.dma('sp', self.X2S[r0:r0 + 128, :], XR[:], [xk], ['X2S%d' % (r0 // 128)], 'st_x2_%d' % xi)
                    self.to_fm(XR, xk, self.XTc, 'XTc', j)
                else:
                    self.dma('sp', self.out[s, r0:r0 + 128, :], XR[:], [xk], [], 'st_out_%d' % xi)
        if l == 0:
            self.dma('sp', self.X2T.rearrange("(k p) t -> p k t", p=P)[:, :, t0:t0 + 512], XTc[:], ['XTc'], ['X2T%d' % qc], 'st_x2t')

    def build(self):
        d = self.d
        self.load_consts()
        if self.stop not in ('consts', 'proj', 'cmp', 'mem', 'l0_noffn'):
            self.precast_ffn_weights()
        for s in range(self.nseq):
            self.load_layer_consts(0)
            if self.stop == 'consts':
                break
            self.proj_pass(lambda qc: d['xT'][s].rearrange("(k p) t -> p k t", p=P)[:, :, qc * 512:(qc + 1) * 512], 'pool',
                           self.WA, 'WA', d['w0'].rearrange("(k p) n -> p k n", p=P), 8, [(8, 0), (9, 1), (10, 2)], self.l0_tok)
            if self.stop in ('proj', 'proj1', 'proj2', 'proj3', 'proj4', 'tok1', 'tok2'):
                break
            self.compress()
            if self.stop == 'cmp':
                break
            self.mem_kv(s, 'a_wmkv')
            if self.stop == 'mem':
                QC, qkey = self.load_q(0)
                self.mem_attn(QC, qkey)
                self.flush_attn()
                break
            self.dma('pool', self.WB[:], d['a_wout'].rearrange("(k p) n -> p k n", p=P), [], ['WB'], 'w_WB')
            for qc in range(4):
                if self.stop != 'l0_noattn':
                    self.nsa_chunk(qc)
                    if qc < 3:
                        self.prefetch_q(qc + 1)
                self.mixer_tail(s, 0, qc)
                if self.stop != 'l0_noffn':
                    self.ffn_chunk(s, 0, qc)
            if self.stop in ('l0', 'l0_noffn', 'l0_noattn'):
                continue
            self.load_layer_consts(1)
            x2src = lambda qc: self.X2T.rearrange("(k p) t -> p k t", p=P)[:, :, qc * 512:(qc + 1) * 512]
            self.proj_pass_l1(x2src)
            self.mem_kv(s, 'b_wmkv')
            self.dma('pool', self.WB[:], d['b_wout'].rearrange("(k p) n -> p k n", p=P), [], ['WB'], 'w_WB')
            for qc in range(4):
                self.moba_chunk(qc)
                if qc < 3:
                    self.prefetch_q(qc + 1)
                self.mixer_tail(s, 1, qc)
                self.ffn_chunk(s, 1, qc)
        self.S.emit(self.nc, self.st)
        return self.nc

    def proj_pass_l1(self, x2src):
        d = self.d
        S = self.S
        deps = ['X2T%d' % q for q in range(4)]

        def xs(qc):
            return x2src(qc)
        self._x2deps = deps
        self.proj_pass_dep(xs, 'sp', self.WA1, 'WA1', d['b_win'].rearrange("(k p) n -> p k n", p=P), 8, [], None)
        self.proj_pass_dep(xs, 'sp', self.WA2, 'WA2', d['skv'].rearrange("(k p) n -> p k n", p=P), 0,
                           [(g, g) for g in range(6)], self.l1_tok)
        KMEANF, KMEAN, PTK = self.KMEANF, self.KMEAN, self.PTK
        kreads = ['PTK%d_%d' % (g, q) for g in range(6) for q in range(4)]
        self.dve(lambda e: e.tensor_reduce(out=KMEANF[:], in_=PTK[:].rearrange("p g (b t) -> p g b t", t=256),
                                           axis=mybir.AxisListType.X, op=ALU.add), kreads, ['KMEANF'])
        self.dve(lambda e: e.tensor_scalar_mul(out=KMEAN[:], in0=KMEANF[:], scalar1=1.0 / 256.0), ['KMEANF'], ['KMEAN'])

    def proj_pass_dep(self, xsrc_fn, xeng, W, wkey, wsrc, ngq, kgroups, tok_fn):
        self.dma('pool', W, wsrc, [], [wkey], 'w_' + wkey)
        for qc in range(4):
            t0 = qc * 512
            ri = self.ring('xch', 2)
            XCH, xkey = self.XCHr[ri], 'XCH%d' % ri
            self.dma(xeng, XCH, xsrc_fn(qc), ['X2T%d' % qc], [xkey], 'x_' + xeng + xkey)
            for g in range(ngq):
                B, bk = self.bank()
                for k in range(8):
                    self.mm(B[:], W[:, k, g * 128:(g + 1) * 128], XCH[:, k, :], k == 0, k == 7, [wkey, xkey], [bk])
                self.scaled_copy(self.QST[:, g, :], B[:], 0.125, [bk], ['QST'])
            if ngq:
                self.dma('sp', self.QS[:, :, t0:t0 + 512].rearrange("g p t -> p g t"), self.QST[:, 0:8, :],
                         ['QST'], ['QS%d' % qc], 'qs_st')
            for (g, slot) in kgroups:
                B, bk = self.bank()
                for k in range(8):
                    self.mm(B[:], W[:, k, g * 128:(g + 1) * 128], XCH[:, k, :], k == 0, k == 7, [wkey, xkey], [bk])
                self.copy(self.PTK[:, slot, t0:t0 + 512], B[:], [bk], ['PTK%d_%d' % (slot, qc)])
            if tok_fn is not None:
                for j in range(4):
                    tok_fn(qc, j, XCH, xkey, W, wkey)


def build_program(nseq, debug=False, stop=None):
    kb = KB(nseq, debug, stop)
    kb.setup()
    with kb.st:
        nc = kb.build()
    return nc


_CACHE = {}


def kernel(**inputs):
    x = np.ascontiguousarray(np.asarray(inputs['x'], np.float32))
    mem = np.ascontiguousarray(np.asarray(inputs['mem'], np.float32))
    B = x.shape[0]
    nseq = B // NCORES
    tabs = _host_tables(inputs['rel_bias'])
    wts = _host_weights(inputs)
    shared = {}
    shared.update(tabs)
    shared.update(wts)
    for k, shp in INPUT_SHAPES.items():
        assert list(shared[k].shape) == shp, (k, shared[k].shape, shp)
    xT = np.ascontiguousarray(x.transpose(0, 2, 1))
    memT = np.ascontiguousarray(mem.transpose(0, 2, 1))
    in_maps = []
    for c in range(NCORES):
        m = dict(shared)
        m['x_tok'] = x[c * nseq:(c + 1) * nseq]
        m['xT'] = xT[c * nseq:(c + 1) * nseq]
        m['memT'] = memT[c * nseq:(c + 1) * nseq]
        in_maps.append(m)
    nc = build_program(nseq)
    res = run_bass_kernel_spmd(nc, in_maps, core_ids=list(range(NCORES)))
    out = np.concatenate([np.asarray(r['out']) for r in res.results], axis=0)
    return out.astype(np.float32)
```

```python
import sys
import math
import numpy as np
from contextlib import ExitStack
import concourse.bass as bass
import concourse.mybir as mybir
from concourse.bass_utils import run_bass_kernel_spmd

F32 = mybir.dt.float32
BF16 = mybir.dt.bfloat16
AF = mybir.ActivationFunctionType
ALU = mybir.AluOpType

P = 128
D = 1024
S_LEN = 2048
NCORES = 8
NEG = -30000.0
ALPHA = 4.0 ** 0.25
LN_EPS = 1e-5
DFF = 2816
NFC = 22
TB_OFF = 384


class Sched:
    def __init__(self):
        self.ops = []
        self.lastw = {}
        self.readers = {}
        self.overlaps = {}

    def set_overlap(self, a, b):
        self.overlaps.setdefault(a, set()).add(b)
        self.overlaps.setdefault(b, set()).add(a)

    def _exp(self, k):
        o = self.overlaps.get(k)
        if o:
            return [k] + list(o)
        return [k]

    def add(self, eng, fns, reads=(), writes=(), dma_key=None, rg=None):
        if callable(fns):
            fns = [fns]
        deps = set()
        force = set()
        if rg is not None:
            for w0 in writes:
                lw = self.lastw.get(w0)
                if lw is not None and self.ops[lw].get('rg') is not None and not (self.ops[lw]['rg'] & rg):
                    force.add(lw)
            deps |= force
        for r0 in reads:
            for r in self._exp(r0):
                w = self.lastw.get(r)
                if w is not None:
                    deps.add(w)
            if r0.startswith('bank'):
                for ridx in self.readers.get(r0, ()):
                    if self.ops[ridx]['eng'] != eng:
                        deps.add(ridx)
        for w0 in writes:
            for w_ in self._exp(w0):
                w = self.lastw.get(w_)
                if w is not None:
                    deps.add(w)
                rl = self.readers.get(w_)
                if rl:
                    deps.update(rl)
        idx = len(self.ops)
        self.ops.append(dict(eng=eng, fns=fns, deps=deps, dma_key=dma_key, sig=None, rg=rg, force=force))
        for r in reads:
            self.readers.setdefault(r, []).append(idx)
        for w_ in writes:
            self.lastw[w_] = idx
            self.readers[w_] = []
        return idx

    def emit(self, nc, stack, final_wait_eng='sp'):
        ops = self.ops
        engh = dict(pe=nc.tensor, act=nc.scalar, dve=nc.vector, pool=nc.gpsimd, sp=nc.sync)
        needed = set()
        for op in ops:
            needed |= op['deps']
        EPOCH = 30000
        esems, ecount, dsems, dcount = {}, {}, {}, {}
        nsem = [0]

        def newsem(name):
            nsem[0] += 1
            return stack.enter_context(nc.semaphore(name))

        for idx, op in enumerate(ops):
            if op['dma_key'] is not None:
                k = op['dma_key']
                if k not in dsems:
                    dsems[k] = newsem("d%d_%s" % (nsem[0], k))
                    dcount[k] = 0
                dcount[k] += 16 * len(op['fns'])
                op['sig'] = (dsems[k], dcount[k])
            elif idx in needed:
                e = op['eng']
                if e not in esems or ecount[e] >= EPOCH:
                    esems[e] = newsem("e%s%d" % (e, nsem[0]))
                    ecount[e] = 0
                ecount[e] += 1
                op['sig'] = (esems[e], ecount[e])
        waited = {e: {} for e in engh}
        nwait = 0
        for idx, op in enumerate(ops):
            e = op['eng']
            h = engh[e]
            isdma = op['dma_key'] is not None
            best = {}
            for d in op['deps']:
                dop = ops[d]
                if dop['eng'] == 'pe' and e == 'pe' and dop['dma_key'] is None and not isdma and d not in op['force']:
                    continue
                sem, c = dop['sig']
                key = id(sem)
                if key not in best or best[key][1] < c:
                    best[key] = (sem, c)
            for key, (sem, c) in best.items():
                if waited[e].get(key, 0) >= c:
                    continue
                h.wait_ge(sem, c)
                nwait += 1
                waited[e][key] = c
            n = len(op['fns'])
            for i, fn in enumerate(op['fns']):
                ins = fn(h)
                if isdma:
                    ins.then_inc(op['sig'][0], 16)
                elif i == n - 1 and op['sig'] is not None:
                    ins.then_inc(op['sig'][0], 1)
        h = engh[final_wait_eng]
        for k, s in dsems.items():
            h.wait_ge(s, dcount[k])
        print("sched: ops=%d sems=%d waits=%d" % (len(ops), nsem[0], nwait), file=sys.stderr)


def _rel_bucket_np(n):
    n = np.maximum(n, 0)
    max_exact = 16
    nf = np.maximum(n, 1).astype(np.float32)
    large = max_exact + (np.log(nf / max_exact) / math.log(128 / max_exact) * (32 - max_exact)).astype(np.int32)
    large = np.minimum(large, 31)
    return np.where(n < max_exact, n, large)


def _host_tables(rel_bias):
    rb = np.asarray(rel_bias, np.float32)
    t = {}
    j = np.arange(128)[:, None]
    v = np.arange(1024)[None, :]
    d = v - TB_OFF - j
    bk = _rel_bucket_np(d)
    tbs = np.empty((12, 128, 1024), np.float32)
    for h in range(12):
        tbs[h] = np.where(d >= 0, rb[bk, h], np.float32(NEG))
    t['tbs'] = tbs
    npr = np.arange(48)[:, None]
    i = np.arange(512)[None, :]
    d = i - 16 * npr + 113
    bk = _rel_bucket_np(d)
    tbc = np.empty((12, 48, 512), np.float32)
    for h in range(12):
        tbc[h] = np.where(d >= 0, rb[bk, h], np.float32(NEG))
    t['tbc'] = tbc
    w = np.arange(896)[None, :]
    t['fm0'] = np.where(w - 384 - j >= 0, np.float32(NEG), np.float32(0.0)).astype(np.float32)
    n = np.arange(128)
    cbcol = np.zeros((128, 48), np.float32)
    for qc in range(4):
        n0 = 32 * qc - 9
        for h in range(12):
            col = np.where(n < n0, rb[31, h], np.where(n < n0 + 48, np.float32(0.0), np.float32(NEG)))
            cbcol[:, qc * 12 + h] = col
    t['cbcol'] = cbcol
    t['crow'] = np.broadcast_to(rb[31][None, :], (128, 12)).astype(np.float32).copy()
    tt = np.arange(16)[None, :, None]
    pp = np.arange(128)[:, None, None]
    tok = tt * 128 + pp
    cur = tok // 64
    bid = np.arange(32)[None, None, :]
    elig = bid <= cur
    forced = (bid == 0) | (bid == cur) | (bid == cur - 1)
    t['forced'] = np.where(elig, np.where(forced, 1.0e4, 0.0), -1.0e30).astype(np.float32)
    start = np.arange(127) * 16
    end = start + 32
    bs = np.arange(32) * 64
    ov = ((start[:, None] < bs[None, :] + 64) & (end[:, None] > bs[None, :])).astype(np.float32)
    ovp = np.zeros((128, 32), np.float32)
    ovp[:127] = ov
    t['ov'] = ovp
    c = np.arange(2048)[None, :]
    e32 = (c // 64 == np.arange(32)[:, None]).astype(np.float32)
    e96 = np.zeros((96, 2048), np.float32)
    e96[0:32] = e32
    e96[64:96] = e32
    t['e32'] = e96
    t['e8'] = (c // 256 == np.arange(8)[:, None]).astype(np.float32)
    cc = np.arange(264)[None, :]
    t['idw'] = (cc == np.arange(48)[:, None] + 128).astype(np.float32)
    cb = np.arange(8)[:, None]
    blk = np.arange(8)[None, :]
    mbneg = np.where(blk >= cb, np.float32(-1.0e30), np.float32(0.0)).reshape(1, 64)
    nvmk = np.where(blk < cb, np.float32(NEG), np.float32(0.0)).reshape(1, 64)
    t['mbneg'] = np.broadcast_to(mbneg, (128, 64)).astype(np.float32).copy()
    t['nvmk'] = np.broadcast_to(nvmk, (128, 64)).astype(np.float32).copy()
    return t


def _host_weights(inp):
    f = lambda a: np.ascontiguousarray(np.asarray(a, np.float32))
    w = {}
    awin = f(inp['a_w_in'])[0]
    seg = lambda a, b: awin[:, a:b]
    w['w0'] = f(np.concatenate([seg(0, 768), seg(1188, 1444), seg(768, 896), seg(896, 960), seg(896, 960),
                                seg(1024, 1088), seg(1024, 1088), seg(960, 1024), seg(1088, 1188)], axis=1))
    assert w['w0'].shape == (1024, 1572)
    w1k = f(inp['a_cmp_w1_k'])[0].reshape(32, 64, 256).transpose(1, 0, 2)
    w1v = f(inp['a_cmp_w1_v'])[0].reshape(32, 64, 256).transpose(1, 0, 2)
    w['w1cmp'] = f(np.concatenate([w1k, w1v], axis=0))
    w2k = f(inp['a_cmp_w2_k'])[0]
    w['w2k'] = f(np.concatenate([w2k, w2k], axis=1))
    w['w2v'] = f(inp['a_cmp_w2_v'])[0]
    w['pet'] = f(np.concatenate([f(inp['a_cmp_pe_k'])[0].T, f(inp['a_cmp_pe_v'])[0].T], axis=0))
    w['a_wmkv'] = f(inp['a_w_mem_kv'])[0]
    w['a_wout'] = f(inp['a_w_out'])[0]
    w['b_win'] = f(inp['b_w_in'])[0]
    w['skv'] = f(inp['shared_w_kv'])
    w['b_wmkv'] = f(inp['b_w_mem_kv'])[0]
    w['b_wout'] = f(inp['b_w_out'])[0]
    lnp = np.stack([np.stack([f(inp['ln1_g'])[l], f(inp['ln1_b'])[l], f(inp['ln2_g'])[l], f(inp['ln2_b'])[l]])
                    for l in range(2)])
    w['lnp'] = f(np.broadcast_to(lnp[:, :, None, :], (2, 4, 128, 1024)))
    fw = f(inp['ffn_w_in'])
    a = fw[:, :, :DFF].reshape(2, 8, 128, NFC, 128)
    b = fw[:, :, DFF:].reshape(2, 8, 128, NFC, 128)
    ab = np.concatenate([a, b], axis=4)
    w['fwin'] = f(ab.transpose(0, 3, 2, 1, 4))
    w['fwout'] = f(inp['ffn_w_out'])
    cw = f(inp['ffn_conv_w'])
    w['cw'] = f(cw.reshape(2, 3, NFC, 128).transpose(0, 3, 2, 1))
    w['cb'] = f(f(inp['ffn_conv_b']).reshape(2, NFC, 128).transpose(0, 2, 1))
    return w


INPUT_SHAPES = dict(
    w0=[1024, 1572], w1cmp=[128, 32, 256], w2k=[256, 128], w2v=[256, 64], pet=[128, 32],
    a_wmkv=[1024, 512], a_wout=[1024, 1024], b_win=[1024, 1024], skv=[1024, 1536],
    b_wmkv=[1024, 512], b_wout=[1024, 1024], lnp=[2, 4, 128, 1024],
    fwin=[2, NFC, 128, 8, 256], fwout=[2, DFF, 1024], cw=[2, 128, NFC, 3], cb=[2, 128, NFC],
    tbs=[12, 128, 1024], tbc=[12, 48, 512], fm0=[128, 896], cbcol=[128, 48], crow=[128, 12],
    forced=[128, 16, 32], ov=[128, 32], e32=[96, 2048], e8=[8, 2048], idw=[48, 264],
    mbneg=[128, 64], nvmk=[128, 64],
)


class KB:
    def __init__(self, nseq, debug=False, stop=None):
        self.nseq = nseq
        self.debug = debug
        self.stop = stop
        self.nc = bass.Bass("TRN2", target_bir_lowering=False)
        self.S = Sched()
        self.st = ExitStack()
        self.views = []
        self._bank_i = 0
        self._obank_i = 0
        self._bank6_i = 0
        self._bankT_i = 0
        self._ev = 0
        self._uid = 0
        self.jobs = []

    def sb(self, name, shape, dt):
        return self.st.enter_context(self.nc.sbuf_tensor(name, shape, dt))

    def din(self, name, shape, dt=F32):
        return self.nc.dram_tensor(name, shape, dt, kind="ExternalInput").ap()

    def view(self, key, off, shape, dt):
        n = 1
        for s_ in shape[1:]:
            n *= s_
        nel = n * (2 if dt == F32 else 1)
        ap = self.SCR[:, off:off + nel]
        if dt == F32:
            ap = ap.bitcast(F32)
        if len(shape) == 3:
            ap = ap.rearrange("p (a b) -> p a b", b=shape[2])
        elif len(shape) == 4:
            ap = ap.rearrange("p (a b c) -> p a b c", b=shape[2], c=shape[3])
        for (k2, o2, e2) in self.views:
            if off < e2 and o2 < off + nel:
                self.S.set_overlap(key, k2)
        self.views.append((key, off, off + nel))
        assert off + nel <= self.SCR_N, (key, off + nel)
        return ap

    def bank(self):
        i = self._bank_i % 4
        self._bank_i += 1
        return self.banks[i], "bank%d" % i

    def bank6(self):
        i = self._bank6_i % 6
        self._bank6_i += 1
        return self.banks[i], "bank%d" % i

    def obank(self):
        i = 4 + (self._obank_i % 2)
        self._obank_i += 1
        return self.banks[i], "bank%d" % i

    def bankT(self):
        i = self._bankT_i % len(self.banksT)
        self._bankT_i += 1
        return self.banksT[i], "bank%d" % (6 + i)

    def ring(self, name, n):
        c = getattr(self, "_r_" + name, 0)
        setattr(self, "_r_" + name, c + 1)
        return c % n

    def mm(self, out, lhsT, rhs, start, stop, reads, writes):
        p0 = lhsT.base_partition()
        kk = lhsT.shape[0]
        rg = frozenset(range(p0 // 32, (p0 + kk + 31) // 32))
        self.S.add('pe', lambda e: e.matmul(out, lhsT=lhsT, rhs=rhs, start=start, stop=stop, skip_group_check=True),
                   reads, writes, rg=rg)

    def tr(self, out, in_, reads, writes):
        ident = self.ident
        self.S.add('pe', lambda e: e.transpose(out, in_, ident[:]), list(reads) + ['ident'], writes)

    def dma(self, eng, out, in_, reads, writes, key):
        if len(out.shape) == 3 and out.shape[0] * out.shape[1] > 2048:
            fns = []
            for a in range(out.shape[1]):
                fns.append(lambda e, a=a: e.dma_start(out=out[:, a, :], in_=in_[:, a, :]))
            self.S.add(eng, fns, reads, writes, dma_key=key)
        else:
            self.S.add(eng, lambda e: e.dma_start(out=out, in_=in_), reads, writes, dma_key=key)

    def act(self, out, in_, func, reads, writes, bias=None, scale=None):
        kw = {}
        if bias is not None:
            kw['bias'] = bias
        if scale is not None:
            kw['scale'] = scale
        self.S.add('act', lambda e: e.activation(out=out, in_=in_, func=func, **kw), reads, writes)

    def copy(self, out, in_, reads, writes, eng=None):
        if eng is None:
            self._ev += 1
            eng = 'act' if self._ev % 2 else 'dve'
        if eng == 'act':
            self.S.add('act', lambda e: e.copy(out=out, in_=in_), reads, writes)
        else:
            self.S.add(eng, lambda e: e.tensor_copy(out=out, in_=in_), reads, writes)

    def scaled_copy(self, out, in_, mul, reads, writes):
        self._ev += 1
        if self._ev % 2:
            self.S.add('act', lambda e: e.mul(out=out, in_=in_, mul=mul), reads, writes)
        else:
            self.S.add('dve', lambda e: e.tensor_scalar_mul(out=out, in0=in_, scalar1=mul), reads, writes)

    def tt(self, out, in0, in1, op, reads, writes, eng='dve'):
        self.S.add(eng, lambda e: e.tensor_tensor(out=out, in0=in0, in1=in1, op=op), reads, writes)

    def dve(self, fn, reads, writes):
        self.S.add('dve', fn, reads, writes)

    def setup(self):
        nc, ns = self.nc, self.nseq
        d = {}
        d['x_tok'] = self.din('x_tok', [ns, S_LEN, D])
        d['xT'] = self.din('xT', [ns, D, S_LEN])
        d['memT'] = self.din('memT', [ns, D, 256])
        for k, shp in INPUT_SHAPES.items():
            d[k] = self.din(k, shp)
        self.d = d
        self.out = nc.dram_tensor('out', [ns, S_LEN, D], F32, kind="ExternalOutput").ap()
        skind = "ExternalOutput" if self.debug else "Internal"
        self.QS = nc.dram_tensor('QS', [8, P, S_LEN], BF16, kind=skind).ap()
        self.X1S = nc.dram_tensor('X1S', [S_LEN, D], F32, kind=skind).ap()
        self.X2S = nc.dram_tensor('X2S', [S_LEN, D], F32, kind=skind).ap()
        self.X2T = nc.dram_tensor('X2T', [D, S_LEN], BF16, kind=skind).ap()
        self.FWINB = nc.dram_tensor('FWINB', [2, NFC, P, 2048], BF16, kind="Internal").ap()
        self.FWOUTB = nc.dram_tensor('FWOUTB', [2, NFC, P, 1024], BF16, kind="Internal").ap()

        sb = self.sb
        self.ident = sb('ident', [P, P], BF16)
        self.identf = sb('identf', [P, P], F32)
        self.TBSr = [sb('TBS%d' % i, [P, 1024], BF16) for i in range(3)]
        self.TBCr = [sb('TBC%d' % i, [48, 512], BF16) for i in range(3)]
        self.FM0 = sb('FM0', [P, 896], BF16)
        self.E32 = sb('E32', [96, 2048], BF16)
        self.IDW = sb('IDW', [48, 264], BF16)
        self.CBCOL = sb('CBCOL', [P, 48], F32)
        self.CROW = sb('CROW', [P, 12], F32)
        self.FORCED = sb('FORCED', [P, 16, 32], F32)
        self.MBNEG = sb('MBNEG', [P, 64], F32)
        self.NVMK = sb('NVMK', [P, 64], F32)
        self.LNP = sb('LNP', [P, 4, 1024], F32)
        self.CW = sb('CW', [P, NFC, 3], F32)
        self.CBt = sb('CBt', [P, NFC], F32)
        self.EPS = sb('EPS', [P, 1], F32)
        self.PTK = sb('PTK', [P, 6, S_LEN], BF16)
        self.QCr = [sb('QC%d' % i, [P, 8, 512], BF16) for i in range(1)]
        self.VB = sb('VB', [P, 16, 12, 65], BF16)
        self.G = sb('G', [P, 16, 36], F32)
        self.KCT = sb('KCT', [P, 128], BF16)
        self.VCA = sb('VCA', [P, 97], BF16)
        self.GK = sb('GK', [P, 2, 128], BF16)
        self.GV = sb('GV', [P, 2, 128], BF16)
        self.W2K = sb('W2K', [P, 2, 128], BF16)
        self.W2V = sb('W2V', [P, 2, 64], BF16)
        self.PET = sb('PET', [P, 32], F32)
        self.KMT = sb('KMT', [P, 2, 256], BF16)
        self.VM = sb('VM', [P, 2, 4, 65], BF16)
        self.KMEANF = sb('KMEANF', [P, 6, 8], F32)
        self.KMEAN = sb('KMEAN', [P, 6, 8], BF16)
        self.WB = sb('WB', [P, 8, 1024], BF16)
        self.XTc = sb('XTc', [P, 8, 512], BF16)
        self.XRr = [sb('XR%d' % i, [P, 1024], F32) for i in range(4)]
        self.XBFr = [sb('XBF%d' % i, [P, 1024], BF16) for i in range(2)]
        self.HALO = sb('HALO', [P, NFC, 2], F32)
        self.RDr = [sb('RD%d' % i, [P, 4, 1], F32) for i in range(3)]
        self.WGr = [sb('WG%d' % i, [P, 4, 1], F32) for i in range(3)]
        self.TMPr = [sb('TMP%d' % i, [P, 4, 64], F32) for i in range(1)]
        self.IMPACC = sb('IMPACC', [P, 4, 32], F32)
        self.IMPT = sb('IMPT', [P, 4, 32], F32)
        self.SC = sb('SC', [P, 4, 32], F32)
        self.M8 = sb('M8', [P, 16], F32)
        self.WK = sb('WK', [P, 32], F32)
        self.PEN = sb('PEN', [P, 4, 96], BF16)
        self.PENT = sb('PENT', [96, 512], BF16)
        self.GM = sb('GM', [P, 12, 8], F32)
        self.M8M = sb('M8M', [P, 12, 8], F32)
        self.LT = sb('LT', [P, 12, 8], F32)
        self.PENM = sb('PENM', [P, 4, 12, 8], BF16)
        self.PENTM = sb('PENTM', [96, 512], BF16)
        self.BNSTr = [sb('BNST%d' % i, [P, 12], F32) for i in range(4)]
        self.MVr = [sb('MV%d' % i, [P, 2], F32) for i in range(4)]
        self.RSTDr = [sb('RSTD%d' % i, [P, 1], F32) for i in range(4)]
        self.NMRr = [sb('NMR%d' % i, [P, 1], F32) for i in range(4)]
        self.NMR = sb('NMR', [P, 1], F32)
        self.SCR_N = 28192
        self.SCR = sb('SCR', [P, self.SCR_N], BF16)
        self.banks = [self.st.enter_context(nc.psum_tensor('pb%d' % i, [P, 512], F32)) for i in range(8)]
        self.banksT = [self.banks[6][:, :].bitcast(BF16), self.banks[7][:, :].bitcast(BF16)]

        V = self.view
        self.XCHr = [V('XCH0', 0, [P, 8, 512], BF16), V('XCH1', 4096, [P, 8, 512], BF16)]
        self.QST = V('QST', 8192, [P, 8, 512], BF16)
        self.WA = V('WA', 12288, [P, 8, 1572], BF16)
        self.WA1 = V('WA1', 12288, [P, 8, 1024], BF16)
        self.WA2 = V('WA2', 12288, [P, 8, 1536], BF16)
        self.W1 = V('W1', 0, [P, 32, 256], BF16)
        self.CA = V('CA', 8192, [P, 16, 128], BF16)
        self.CBv = V('CBv', 10240, [P, 16, 128], BF16)
        self.MEMT = V('MEMT', 12288, [P, 8, 256], BF16)
        self.WMKV = V('WMKV', 14336, [P, 8, 512], BF16)
        self.OACC = V('OACC', 0, [P, 4, 768], F32)
        self.OB = V('OB', 6144, [P, 4, 1024], BF16)
        self.OT = V('OT', 10240, [P, 8, 512], BF16)
        self.PRr = [V('PR%d' % i, 14336 + 512 * i, [P, 512], BF16) for i in range(4)]
        self.SAr = [V('SA%d' % i, 16384 + 1024 * i, [P, 512], F32) for i in range(3)]
        self.HT = V('HT', 0, [P, NFC, 512], BF16)
        self.ACr = [V('AC0', 11264, [P, 514], F32), V('AC1', 12296, [P, 514], F32), V('AC2', 25616, [P, 514], F32)]
        self.T1r = [V('T10', 13328, [P, 512], F32), V('T11', 14352, [P, 512], F32), V('T12', 26648, [P, 512], F32)]
        self.GBr = [V('GB0', 15376, [P, 512], BF16), V('GB1', 15888, [P, 512], BF16), V('GB2', 27672, [P, 512], BF16)]
        self.WINr = [V('WIN%d' % i, 16400 + 2048 * i, [P, 8, 256], BF16) for i in range(3)]
        self.WOUTr = [V('WOUT%d' % i, 22544 + 1024 * i, [P, 1024], BF16) for i in range(3)]
        subs = {'AC0': ['AC0h'], 'AC1': ['AC1h'], 'AC2': ['AC2h'],
                'OACC': ['OACC%d' % h for h in range(12)],
                'OB': ['OBm%d' % m for m in range(4)] + ['OBh%d' % h for h in range(12)] + ['OBmain'],
                'OT': ['OT%d' % j for j in range(4)]}
        for base, sl in subs.items():
            for sk in sl:
                for o in list(self.S.overlaps.get(base, ())):
                    self.S.set_overlap(sk, o)
        for h in range(12):
            self.S.set_overlap('OBmain', 'OBh%d' % h)

    def load_consts(self):
        S, d = self.S, self.d
        identf, ident = self.identf, self.ident
        S.add('pool', lambda e: e.memset(identf[:], 0.0), writes=['identf'])
        S.add('pool', lambda e: e.affine_select(out=identf[:], in_=identf[:], pattern=[[-1, P]],
                                                compare_op=ALU.not_equal, fill=1.0, base=0, channel_multiplier=1),
              reads=['identf'], writes=['identf'])
        S.add('dve', lambda e: e.tensor_copy(out=ident[:], in_=identf[:]), reads=['identf'], writes=['ident'])
        EPS, VB, VM, VCA = self.EPS, self.VB, self.VM, self.VCA
        S.add('dve', lambda e: e.memset(EPS[:], LN_EPS), writes=['EPS'])
        PEN_ = self.PEN
        S.add('dve', lambda e: e.memset(PEN_[:], 0.0), writes=['PEN'])
        S.add('dve', lambda e: e.memset(VB[:, :, :, 64:65], 1.0), writes=['VBones'])
        S.add('dve', lambda e: e.memset(VM[:, :, :, 64:65], 1.0), writes=['VMones'])
        S.add('dve', lambda e: e.memset(VCA[:, 64:65], 1.0), writes=['VCAones'])
        cast = [('FM0', self.FM0[:], d['fm0']), ('E32', self.E32[:], d['e32']),
                ('IDW', self.IDW[:], d['idw']), ('VCAov', self.VCA[:, 65:97], d['ov']),
                ('W2K', self.W2K[:], d['w2k'].rearrange("(c p) n -> p c n", p=P)),
                ('W2V', self.W2V[:], d['w2v'].rearrange("(c p) n -> p c n", p=P))]
        for k, o, i in cast:
            self.dma('pool', o, i, [], [k], 'c_' + k)
        plain = [('CBCOL', self.CBCOL[:], d['cbcol']), ('CROW', self.CROW[:], d['crow']),
                 ('FORCED', self.FORCED[:], d['forced']), ('MBNEG', self.MBNEG[:], d['mbneg']),
                 ('NVMK', self.NVMK[:], d['nvmk']), ('PET', self.PET[:], d['pet'])]
        for k, o, i in plain:
            self.dma('sp', o, i, [], [k], 'c_' + k)

    def precast_ffn_weights(self):
        d = self.d
        for l in range(2):
            for c in range(NFC):
                wi = self.ring('win', 3)
                WIN, wk = self.WINr[wi], 'WIN%d' % wi
                self.dma('pool', WIN.rearrange("p k n -> p (k n)"), d['fwin'][l, c].rearrange("p k n -> p (k n)"), [], [wk], 'f_' + wk)
                self.dma('sp', self.FWINB[l, c], WIN.rearrange("p k n -> p (k n)"), [wk], ['FWINB'], 'pc_' + wk)
                wo = self.ring('wout', 3)
                WO, wok = self.WOUTr[wo], 'WOUT%d' % wo
                self.dma('pool', WO, d['fwout'][l, c * 128:(c + 1) * 128, :], [], [wok], 'f_' + wok)
                self.dma('sp', self.FWOUTB[l, c], WO, [wok], ['FWOUTB'], 'pc_' + wok)

    def load_layer_consts(self, l):
        d = self.d
        self.dma('sp', self.LNP[:], d['lnp'][l].rearrange("f p n -> p f n"), [], ['LNP'], 'c_LNP')
        self.dma('sp', self.CW[:], d['cw'][l], [], ['CW'], 'c_CW')
        self.dma('sp', self.CBt[:], d['cb'][l], [], ['CBt'], 'c_CBt')
        HALO = self.HALO
        self.S.add('pool', lambda e: e.memset(HALO[:], 0.0), reads=[], writes=['HALO%d' % c for c in range(NFC)])

    def proj_pass(self, xsrc_fn, xeng, W, wkey, wsrc, ngroups_q, kgroups, tok_fn):
        self.dma('pool', W, wsrc, [], [wkey], 'w_' + wkey)
        lvl = {'proj1': 1, 'proj2': 2, 'proj3': 3, 'proj4': 4}.get(self.stop, 9)
        for qc in range(4):
            t0 = qc * 512
            ri = self.ring('xch', 2)
            XCH, xkey = self.XCHr[ri], 'XCH%d' % ri
            self.dma(xeng, XCH, xsrc_fn(qc), [], [xkey], 'x_' + xeng + xkey)
            if lvl < 2:
                continue
            for g in range(ngroups_q):
                B, bk = self.bank()
                for k in range(8):
                    self.mm(B[:], W[:, k, g * 128:(g + 1) * 128], XCH[:, k, :], k == 0, k == 7, [wkey, xkey], [bk])
                self.scaled_copy(self.QST[:, g, :], B[:], 0.125, [bk], ['QST'])
            if ngroups_q and lvl >= 3:
                self.dma('sp', self.QS[:, :, t0:t0 + 512].rearrange("g p t -> p g t"), self.QST[:, 0:8, :],
                         ['QST'], ['QS%d' % qc], 'qs_st')
            if lvl < 4:
                continue
            for (g, slot) in kgroups:
                B, bk = self.bank()
                for k in range(8):
                    self.mm(B[:], W[:, k, g * 128:(g + 1) * 128], XCH[:, k, :], k == 0, k == 7, [wkey, xkey], [bk])
                self.copy(self.PTK[:, slot, t0:t0 + 512], B[:], [bk], ['PTK%d_%d' % (slot, qc)])
            if tok_fn is not None and lvl >= 5:
                for j in range(4):
                    tok_fn(qc, j, XCH, xkey, W, wkey)

    def l0_tok(self, qc, j, XCH, xkey, W, wkey):
        tt = 4 * qc + j
        B, bk = self.bank()
        for k in range(8):
            self.mm(B[:, 0:164], XCH[:, k, j * 128:(j + 1) * 128], W[:, k, 1408:1572], k == 0, k == 7, [wkey, xkey], [bk])
        if self.stop == 'tok1':
            return
        self.copy(self.VB[:, tt, 0:2, 0:64], B[:, 0:128].rearrange("p (a b) -> p a b", b=64), [bk], ['VB%d' % tt])
        if self.stop == 'tok2':
            return
        G = self.G
        self.act(G[:, tt, :], B[:, 128:164], AF.Exp, [bk], ['G%d' % tt], scale=-1.0)
        self.dve(lambda e: e.tensor_scalar_add(out=G[:, tt, :], in0=G[:, tt, :], scalar1=1.0), ['G%d' % tt], ['G%d' % tt])
        self.dve(lambda e: e.reciprocal(out=G[:, tt, :], in_=G[:, tt, :]), ['G%d' % tt], ['G%d' % tt])

    def l1_tok(self, qc, j, XCH, xkey, W, wkey):
        tt = 4 * qc + j
        B, bk = self.bank()
        B2, bk2 = self.bank()
        for k in range(8):
            self.mm(B[:], XCH[:, k, j * 128:(j + 1) * 128], W[:, k, 768:1280], k == 0, k == 7, [wkey, xkey], [bk])
        for k in range(8):
            self.mm(B2[:, 0:256], XCH[:, k, j * 128:(j + 1) * 128], W[:, k, 1280:1536], k == 0, k == 7, [wkey, xkey], [bk2])
        self.copy(self.VB[:, tt, 0:8, 0:64], B[:].rearrange("p (a b) -> p a b", b=64), [bk], ['VB%d' % tt])
        self.copy(self.VB[:, tt, 8:12, 0:64], B2[:, 0:256].rearrange("p (a b) -> p a b", b=64), [bk2], ['VB%d' % tt])

    def compress(self):
        d = self.d
        W1, CA, CBv, PET, PTK = self.W1, self.CA, self.CBv, self.PET, self.PTK
        self.dma('pool', W1, d['w1cmp'], [], ['W1'], 'w_W1')
        src = PTK[:, 0, :].rearrange("p (i l) -> p l i", l=16)
        kreads = ['PTK0_%d' % q for q in range(4)]
        self.tt(CA, src, PET[:, 0:16].unsqueeze(2).to_broadcast([P, 16, 128]), ALU.add, kreads + ['PET'], ['CA'])
        self.tt(CBv, src, PET[:, 16:32].unsqueeze(2).to_broadcast([P, 16, 128]), ALU.add, kreads + ['PET'], ['CBv'])
        for (pb, GX, gkey) in ((0, self.GK, 'GK'), (64, self.GV, 'GV')):
            for hc in range(2):
                B, bk = self.bank()
                for l in range(32):
                    rhs = CA[pb:pb + 64, l, 0:127] if l < 16 else CBv[pb:pb + 64, l - 16, 1:128]
                    self.mm(B[:, 0:127], W1[pb:pb + 64, l, hc * 128:(hc + 1) * 128], rhs, l == 0, l == 31,
                            ['W1', 'CA', 'CBv'], [bk])
                self.act(GX[:, hc, 0:127], B[:, 0:127], AF.Gelu_apprx_tanh, [bk], [gkey])
        B, bk = self.bank()
        for c in range(2):
            self.mm(B[:, 0:127], self.W2K[:, c, :], self.GK[:, c, 0:127], c == 0, c == 1, ['W2K', 'GK'], [bk])
        self.copy(self.KCT[:, 0:127], B[:, 0:127], [bk], ['KCT'])
        B, bk = self.bank()
        for c in range(2):
            self.mm(B[0:127, 0:64], self.GV[:, c, 0:127], self.W2V[:, c, :], c == 0, c == 1, ['W2V', 'GV'], [bk])
        self.copy(self.VCA[0:127, 0:64], B[0:127, 0:64], [bk], ['VCAv'])

    def mem_kv(self, s, wname):
        d = self.d
        MEMT, WMKV = self.MEMT, self.WMKV
        self.dma('pool', MEMT, d['memT'][s].rearrange("(k p) m -> p k m", p=P), [], ['MEMT'], 'w_MEMT')
        self.dma('pool', WMKV, d[wname].rearrange("(k p) n -> p k n", p=P), [], ['WMKV'], 'w_WMKV')
        for g in range(2):
            B, bk = self.bank()
            for k in range(8):
                self.mm(B[:, 0:256], WMKV[:, k, g * 128:(g + 1) * 128], MEMT[:, k, :], k == 0, k == 7, ['MEMT', 'WMKV'], [bk])
            self.copy(self.KMT[:, g, :], B[:, 0:256], [bk], ['KMT'])
        for mt in range(2):
            B, bk = self.bank()
            for k in range(8):
                self.mm(B[:, 0:256], MEMT[:, k, mt * 128:(mt + 1) * 128], WMKV[:, k, 256:512], k == 0, k == 7,
                        ['MEMT', 'WMKV'], [bk])
            self.copy(self.VM[:, mt, :, 0:64], B[:, 0:256].rearrange("p (a b) -> p a b", b=64), [bk], ['VMv'])

    def attn(self, q_ap, q_reads, ktiles, ncols, post, pre=None):
        first, last = {}, {}
        for i, kt in enumerate(ktiles):
            for j in kt['js']:
                first.setdefault(j, i)
                last[j] = i
        assert len(first) == 4
        self.jobs.append(dict(q=q_ap, qr=list(q_reads), kts=ktiles, ncols=ncols, post=post, pre=pre, last=last,
                              O=None, pv_started=False, pre_done=False))

    def flush_attn(self, D=3):
        jobs = self.jobs
        self.jobs = []
        items = []
        for jn, job in enumerate(jobs):
            for i in range(len(job['kts'])):
                items.append((jn, i))
        N = len(items)
        staged = {}
        for idx in range(N + D):
            if idx < N:
                jn, i = items[idx]
                job = jobs[jn]
                if i == 0:
                    for jj in range(jn, min(jn + 3, len(jobs))):
                        if jobs[jj]['pre'] is not None and not jobs[jj]['pre_done']:
                            jobs[jj]['pre']()
                            jobs[jj]['pre_done'] = True
                    Ob, okey = self.obank()
                    nco = job['ncols']
                    job['O'] = (Ob[:, 0:4 * nco].rearrange("p (j c) -> p j c", c=nco), okey)
                kt = job['kts'][i]
                Sb, skey = self.bank()
                nk = kt['nk']
                js = kt['js']
                assert js == list(range(js[0], js[-1] + 1))
                qlo, qhi = js[0] * 128, (js[-1] + 1) * 128
                dve_bias = [x for x in kt['extras'] if x[0] is self.ident[:] or getattr(x[0], '_is_ident', False)]
                dve_bias = [x for x in kt['extras'] if x[2] and x[2][0] == 'ident' and x[2][1].startswith('TBS')]
                pe_extras = [x for x in kt['extras'] if x not in dve_bias]
                mms = [(kt['k'][0], job['q'], list(kt['k'][1]) + job['qr'])] + pe_extras
                for m, (l, r, rd) in enumerate(mms):
                    self.mm(Sb[0:nk, qlo:qhi], l, r[:, qlo:qhi], m == 0, m == len(mms) - 1, rd, [skey])
                pi = self.ring('pr', 4)
                Pb, pkey = self.PRr[pi], 'PR%d' % pi
                src_ap, src_key = Sb, skey
                if dve_bias:
                    si2 = self.ring('sadd', 3)
                    SA, sakey = self.SAr[si2], 'SA%d' % si2
                    (_, tb, trd) = dve_bias[0]
                    self.tt(SA[0:nk, qlo:qhi], Sb[0:nk, qlo:qhi], tb[:, qlo:qhi], ALU.add, [skey, trd[1]], [sakey])
                    src_ap, src_key = SA, sakey
                Sb, skey = src_ap, src_key
                if kt['bias'] is not None:
                    self.act(Pb[0:nk, qlo:qhi], Sb[0:nk, qlo:qhi], AF.Exp, [skey] + list(kt['bias'][1]), [pkey],
                             bias=kt['bias'][0])
                else:
                    self.act(Pb[0:nk, qlo:qhi], Sb[0:nk, qlo:qhi], AF.Exp, [skey], [pkey])
                staged[idx] = (Pb, pkey)
            k = idx - D
            if k >= 0:
                jn, i = items[k]
                job = jobs[jn]
                kt = job['kts'][i]
                nk = kt['nk']
                Pb, pkey = staged.pop(k)
                Ov, okey = job['O']
                for j in kt['js']:
                    self.mm(Ov[:, j, :], Pb[0:nk, j * 128:(j + 1) * 128], kt['v'][0], not job['pv_started'],
                            job['last'][j] == i, [pkey] + list(kt['v'][1]), [okey])
                    job['pv_started'] = True
                if i == len(job['kts']) - 1:
                    job['post'](Ov, okey)

    def norm(self, Ov, okey, gate, dst, dkey, add, imp=None):
        ri = self.ring('rd', 3)
        RD, rkey = self.RDr[ri], 'RD%d' % ri
        self.dve(lambda e: e.tensor_scalar_max(out=RD[:], in0=Ov[:, :, 64:65], scalar1=1e-30), [okey], [rkey])
        self.dve(lambda e: e.reciprocal(out=RD[:], in_=RD[:]), [rkey], [rkey])
        if imp is not None:
            first = imp
            if first:
                self.tt(self.IMPACC[:], Ov[:, :, 65:97], RD[:].to_broadcast([P, 4, 32]), ALU.mult, [okey, rkey], ['IMPACC'])
            else:
                self.tt(self.IMPT[:], Ov[:, :, 65:97], RD[:].to_broadcast([P, 4, 32]), ALU.mult, [okey, rkey], ['IMPT'])
                self.tt(self.IMPACC[:], self.IMPACC[:], self.IMPT[:], ALU.add, ['IMPACC', 'IMPT'], ['IMPACC'])
        W, wkey = RD, rkey
        if gate is not None:
            wi = self.ring('wg', 3)
            W, wkey = self.WGr[wi], 'WG%d' % wi
            self.tt(W[:], RD[:], gate[0], ALU.mult, [rkey] + list(gate[1]), [wkey])
        if not add:
            self.tt(dst, Ov[:, :, 0:64], W[:].to_broadcast([P, 4, 64]), ALU.mult, [okey, wkey], [dkey])
        else:
            ti = self.ring('tmp', 1)
            T, tkey = self.TMPr[ti], 'TMP%d' % ti
            self.tt(T[:], Ov[:, :, 0:64], W[:].to_broadcast([P, 4, 64]), ALU.mult, [okey, wkey], [tkey])
            self.tt(dst, dst, T[:], ALU.add, [dkey, tkey], [dkey])

    def load_q(self, qc):
        ri = self.ring('qc', 1)
        QC, qkey = self.QCr[ri], 'QC%d' % ri
        t0 = qc * 512
        self.dma('sp', QC[:], self.QS[:, :, t0:t0 + 512].rearrange("g p t -> p g t"), ['QS%d' % qc], [qkey], 'q_' + qkey)
        return QC, qkey

    def get_q(self, qc):
        if getattr(self, '_qpre', None) is not None and self._qpre[0] == qc:
            r = self._qpre[1]
            self._qpre = None
            return r
        return self.load_q(qc)

    def prefetch_q(self, qc):
        self._qpre = (qc, self.load_q(qc))

    def mem_attn(self, QC, qkey):
        for mh in range(4):
            pb = (mh % 2) * 64
            g = 6 + mh // 2
            kts = []
            for mt in range(2):
                kts.append(dict(k=(self.KMT[pb:pb + 64, mh // 2, mt * 128:(mt + 1) * 128], ['KMT']), extras=[], bias=None,
                                nk=128, v=(self.VM[:, mt, mh, :], ['VMv', 'VMones']), js=[0, 1, 2, 3]))
            def post(Ov, okey, mh=mh):
                self.norm(Ov, okey, None, self.OB[:, :, 768 + mh * 64:768 + (mh + 1) * 64], 'OBm%d' % mh, False)
            self.attn(QC[pb:pb + 64, g, :], [qkey], kts, 65, post)

    def nsa_chunk(self, qc):
        d = self.d
        t0 = qc * 512
        QC, qkey = self.get_q(qc)
        gkeys = ['G%d' % (4 * qc + j) for j in range(4)]
        need_sel = qc >= 2
        n0 = 32 * qc - 9
        for h in range(12):
            pb, g = (h % 2) * 64, h // 2
            ci = self.ring('tbc', 3)
            TBC, ckey = self.TBCr[ci], 'TBC%d' % ci
            def pre(TBC=TBC, ckey=ckey, h=h):
                self.dma('pool', TBC[:], d['tbc'][h], [], [ckey], 't_' + ckey)
            kt = dict(k=(self.KCT[pb:pb + 64, 0:127], ['KCT']),
                      extras=[(self.IDW[0:48, 128 - n0:128 - n0 + 127], TBC[0:48, :], ['IDW', ckey])],
                      bias=(self.CBCOL[0:127, qc * 12 + h:qc * 12 + h + 1], ['CBCOL']), nk=127,
                      v=(self.VCA[0:127, 0:97], ['VCAv', 'VCAones', 'VCAov']), js=[0, 1, 2, 3])
            def post(Ov, okey, h=h):
                gate = (self.G[:, 4 * qc:4 * qc + 4, 3 * h:3 * h + 1], gkeys)
                self.norm(Ov, okey, gate, self.OACC[:, :, h * 64:(h + 1) * 64], 'OACC%d' % h, False,
                          imp=((h == 0) if need_sel else None))
            self.attn(QC[pb:pb + 64, g, :], [qkey], [kt], 97, post, pre)
        self.flush_attn()
        if need_sel:
            SC, M8, WK, PEN, PENT = self.SC, self.M8, self.WK, self.PEN, self.PENT
            self.tt(SC[:], self.IMPACC[:], self.FORCED[:, 4 * qc:4 * qc + 4, :], ALU.add, ['IMPACC', 'FORCED'], ['SC'])
            for j in range(4):
                def f1(e, j=j):
                    return e.max(out=M8[:, 0:8], in_=SC[:, j, :])
                def f2(e, j=j):
                    return e.match_replace(out=WK[:], in_to_replace=M8[:, 0:8], in_values=SC[:, j, :], imm_value=-1e30)
                def f3(e, j=j):
                    return e.max(out=M8[:, 8:16], in_=WK[:])
                def f4(e, j=j):
                    return e.tensor_scalar(out=PEN[:, j, 0:32], in0=SC[:, j, :], scalar1=M8[:, 15:16], scalar2=NEG,
                                           op0=ALU.is_lt, op1=ALU.mult)
                def f5(e, j=j):
                    return e.tensor_scalar(out=PEN[:, j, 64:96], in0=SC[:, j, :], scalar1=M8[:, 15:16], scalar2=NEG,
                                           op0=ALU.is_lt, op1=ALU.mult)
                self.dve(f1, ['SC'], ['M8a'])
                self.dve(f2, ['SC', 'M8a'], ['WK'])
                self.dve(f3, ['WK'], ['M8b'])
                self.dve(f4, ['SC', 'M8b'], ['PEN'])
                self.dve(f5, ['SC', 'M8b'], ['PEN'])
            BT, btk = self.bankT()
            for j in range(4):
                self.tr(BT[0:96, j * 128:(j + 1) * 128], PEN[:, j, :], ['PEN'], [btk])
            self.copy(PENT[:], BT[0:96, 0:512], [btk], ['PENT'])
        for h in range(12):
            pb, g = (h % 2) * 64, h // 2
            si = self.ring('tbs', 3)
            TBS, tkey = self.TBSr[si], 'TBS%d' % si
            def pre(TBS=TBS, tkey=tkey, h=h):
                self.dma('pool', TBS[:], d['tbs'][h], [], [tkey], 't_' + tkey)
            crow = (self.CROW[:, h:h + 1], ['CROW'])
            q_ap = QC[pb:pb + 64, g, :]
            kts = []
            for kt_ in range(0, 4 * qc + 4):
                rel = t0 - 128 * kt_
                ex = []
                if rel <= 128:
                    ex.append((self.ident[:], TBS[:, rel + TB_OFF:rel + TB_OFF + 512], ['ident', tkey]))
                if need_sel:
                    ex.append((self.E32[pb:pb + 32, kt_ * 128:(kt_ + 1) * 128], self.PENT[pb:pb + 32, :], ['E32', 'PENT']))
                kts.append(dict(k=(self.PTK[pb:pb + 64, 1, kt_ * 128:(kt_ + 1) * 128], ['PTK1_%d' % (kt_ // 4)]), extras=ex,
                                bias=(crow if rel >= 256 else None), nk=128,
                                v=(self.VB[:, kt_, 0, :], ['VB%d' % kt_, 'VBones']),
                                js=[j for j in range(4) if 4 * qc + j >= kt_]))
            def post(Ov, okey, h=h):
                gate = (self.G[:, 4 * qc:4 * qc + 4, 3 * h + 1:3 * h + 2], gkeys)
                self.norm(Ov, okey, gate, self.OACC[:, :, h * 64:(h + 1) * 64], 'OACC%d' % h, True)
            self.attn(q_ap, [qkey], kts, 65, post, pre)
            kts = []
            for kt_ in range(max(0, 4 * qc - 4), 4 * qc + 4):
                rel = t0 - 128 * kt_
                ex = []
                if rel <= 128:
                    ex.append((self.ident[:], TBS[:, rel + TB_OFF:rel + TB_OFF + 512], ['ident', tkey]))
                if rel >= 128:
                    ex.append((self.ident[:], self.FM0[:, rel - 128:rel - 128 + 512], ['ident', 'FM0']))
                js = [j for j in range(4) if 4 * qc + j >= kt_ and rel + 128 * j - 127 < 512]
                kts.append(dict(k=(self.PTK[pb:pb + 64, 2, kt_ * 128:(kt_ + 1) * 128], ['PTK2_%d' % (kt_ // 4)]), extras=ex,
                                bias=(crow if rel >= 256 else None), nk=128,
                                v=(self.VB[:, kt_, 1, :], ['VB%d' % kt_, 'VBones']), js=js))
            def post(Ov, okey, h=h):
                gate = (self.G[:, 4 * qc:4 * qc + 4, 3 * h + 2:3 * h + 3], gkeys)
                self.norm(Ov, okey, gate, self.OACC[:, :, h * 64:(h + 1) * 64], 'OACC%d' % h, True)
            self.attn(q_ap, [qkey], kts, 65, post)
        self.mem_attn(QC, qkey)
        self.flush_attn()
        self.copy(self.OB[:, :, 0:768], self.OACC[:], ['OACC%d' % h for h in range(12)], ['OBmain'], eng='act')

    def moba_chunk(self, qc):
        d = self.d
        t0 = qc * 512
        QC, qkey = self.get_q(qc)
        GM, M8M, LT, PENM, PENTM = self.GM, self.M8M, self.LT, self.PENM, self.PENTM
        for j in range(4):
            tt = 4 * qc + j
            cb = tt // 2
            B, bk = self.bank()
            Gv = B[:, 0:96].rearrange("p (h b) -> p h b", b=8)
            for h in list(range(0, 12, 2)) + list(range(1, 12, 2)):
                pb, g = (h % 2) * 64, h // 2
                self.mm(Gv[:, h, :], QC[pb:pb + 64, g, j * 128:(j + 1) * 128], self.KMEAN[pb:pb + 64, g, :], True, True,
                        [qkey, 'KMEAN'], [bk])
            self.tt(GM[:], Gv, self.MBNEG[:, cb * 8:(cb + 1) * 8].unsqueeze(1).to_broadcast([P, 12, 8]), ALU.add,
                    [bk, 'MBNEG'], ['GM'])
            for h in range(12):
                def fm(e, h=h):
                    return e.max(out=M8M[:, h, :], in_=GM[:, h, :])
                self.dve(fm, ['GM'], ['M8M'])
            self.tt(LT[:], GM[:], M8M[:, :, 2:3].to_broadcast([P, 12, 8]), ALU.is_lt, ['GM', 'M8M'], ['LT'])
            self.tt(PENM[:, j, :, :], LT[:], self.NVMK[:, cb * 8:(cb + 1) * 8].unsqueeze(1).to_broadcast([P, 12, 8]),
                    ALU.mult, ['LT', 'NVMK'], ['PENM'])
        BT, btk = self.bankT()
        for j in range(4):
            self.tr(BT[0:96, j * 128:(j + 1) * 128], PENM[:, j, :, :].rearrange("p h b -> p (h b)"), ['PENM'], [btk])
        self.copy(PENTM[:, :], BT[0:96, 0:512], [btk], ['PENTM'])
        for h in range(12):
            pb, g = (h % 2) * 64, h // 2
            si = self.ring('tbs', 3)
            TBS, tkey = self.TBSr[si], 'TBS%d' % si
            def pre(TBS=TBS, tkey=tkey, h=h):
                self.dma('pool', TBS[:], d['tbs'][h], [], [tkey], 't_' + tkey)
            crow = (self.CROW[:, h:h + 1], ['CROW'])
            kts = []
            for kt_ in range(0, 4 * qc + 4):
                rel = t0 - 128 * kt_
                b = kt_ // 2
                ex = []
                if rel <= 128:
                    ex.append((self.ident[:], TBS[:, rel + TB_OFF:rel + TB_OFF + 512], ['ident', tkey]))
                if b < 2 * qc + 1:
                    ex.append((self.ident[0:96, 8 * h + b:8 * h + b + 1].to_broadcast([96, 128]), PENTM[:, :],
                               ['ident', 'PENTM']))
                kts.append(dict(k=(self.PTK[pb:pb + 64, g, kt_ * 128:(kt_ + 1) * 128], ['PTK%d_%d' % (g, kt_ // 4)]), extras=ex,
                                bias=(crow if rel >= 256 else None), nk=128,
                                v=(self.VB[:, kt_, h, :], ['VB%d' % kt_, 'VBones']),
                                js=[j for j in range(4) if 4 * qc + j >= kt_]))
            def post(Ov, okey, h=h):
                self.norm(Ov, okey, None, self.OB[:, :, h * 64:(h + 1) * 64], 'OBh%d' % h, False)
            self.attn(QC[pb:pb + 64, g, :], [qkey], kts, 65, post, pre)
        self.mem_attn(QC, qkey)
        self.flush_attn()

    def ln_ops(self, XR, xk, halves, lidx):
        li = self.ring('lnsm', 4)
        BNST, MV, RSTD = self.BNSTr[li], self.MVr[li], self.RSTDr[li]
        kb, km, kr = 'BNST%d' % li, 'MV%d' % li, 'RSTD%d' % li
        EPS, LNP = self.EPS, self.LNP
        ops = []
        for hf, (B, bk) in enumerate(halves):
            def f(e, hf=hf, B=B):
                return e.scalar_tensor_tensor(out=XR[:, hf * 512:(hf + 1) * 512], in0=XR[:, hf * 512:(hf + 1) * 512],
                                              scalar=ALPHA, in1=B, op0=ALU.mult, op1=ALU.add)
            ops.append(lambda f=f, bk=bk: self.dve(f, [xk, bk], [xk]))
        ops.append(lambda: self.dve(lambda e: e.bn_stats(out=BNST[:, 0:6], in_=XR[:, 0:512]), [xk], [kb]))
        ops.append(lambda: self.dve(lambda e: e.bn_stats(out=BNST[:, 6:12], in_=XR[:, 512:1024]), [xk], [kb]))
        ops.append(lambda: self.dve(lambda e: e.bn_aggr(out=MV[:], in_=BNST[:]), [kb], [km]))
        ops.append(lambda: self.act(RSTD[:], MV[:, 1:2], AF.Sqrt, [km, 'EPS'], [kr], bias=EPS[:]))
        ops.append(lambda: self.dve(lambda e: e.reciprocal(out=RSTD[:], in_=RSTD[:]), [kr], [kr]))
        NMR = self.NMRr[li]
        kn = 'NMR%d' % li
        ops.append(lambda: self.dve(lambda e: e.scalar_tensor_tensor(out=NMR[:], in0=MV[:, 0:1], scalar=-1.0, in1=RSTD[:],
                                                                     op0=ALU.mult, op1=ALU.mult), [km, kr], [kn]))
        ops.append(lambda: self.act(XR[:], XR[:], AF.Identity, [xk, kn, kr], [xk], bias=NMR[:], scale=RSTD[:]))
        ops.append(lambda: self.tt(XR[:], XR[:], LNP[:, lidx, :], ALU.mult, [xk, 'LNP'], [xk]))
        ops.append(lambda: self.tt(XR[:], XR[:], LNP[:, lidx + 1, :], ALU.add, [xk, 'LNP'], [xk]))
        return ops

    @staticmethod
    def interleave(lists):
        n = max(len(l) for l in lists)
        for k in range(n):
            for l in lists:
                if k < len(l):
                    l[k]()

    def to_fm(self, XR, xk, dst, dkey, j):
        bi = self.ring('xbf', 2)
        XBF, bfk = self.XBFr[bi], 'XBF%d' % bi
        self.copy(XBF[:], XR[:], [xk], [bfk], eng='act')
        BT, btk = self.bankT()
        for f in range(8):
            self.tr(BT[:, f * 128:(f + 1) * 128], XBF[:, f * 128:(f + 1) * 128], [bfk], [btk])
        self.copy(dst[:, :, j * 128:(j + 1) * 128], BT[:].rearrange("p (a b) -> p a b", b=128), [btk], [dkey])

    def mixer_tail(self, s, l, qc):
        t0 = qc * 512
        obk = ['OBmain'] + ['OBm%d' % m for m in range(4)] if l == 0 else \
              ['OBh%d' % h for h in range(12)] + ['OBm%d' % m for m in range(4)]
        for j in range(4):
            BT, btk = self.bankT()
            for f in range(8):
                self.tr(BT[:, f * 128:(f + 1) * 128], self.OB[:, j, f * 128:(f + 1) * 128], obk, [btk])
            self.copy(self.OT[:, :, j * 128:(j + 1) * 128], BT[:].rearrange("p (a b) -> p a b", b=128), [btk], ['OT%d' % j])
        allb = [(self.banks[b], 'bank%d' % b) for b in range(8)]
        for pair in ((0, 1, 2, 3),):
            chains, info = [], []
            for j in pair:
                r0 = t0 + j * 128
                xi = self.ring('xr', 4)
                XR, xk = self.XRr[xi], 'XR%d' % xi
                if l == 0:
                    self.dma('sp', XR[:], self.d['x_tok'][s, r0:r0 + 128, :], [], [xk], 'r_' + xk)
                else:
                    self.dma('sp', XR[:], self.X2S[r0:r0 + 128, :], ['X2S%d' % (r0 // 128)], [xk], 'r_' + xk)
                halves = []
                for hf in range(2):
                    B, bk = allb[j * 2 + hf]
                    for f in range(8):
                        self.mm(B[:], self.OT[:, f, j * 128:(j + 1) * 128], self.WB[:, f, hf * 512:(hf + 1) * 512], f == 0,
                                f == 7, ['OT%d' % j, 'WB'], [bk])
                    halves.append((B[:], bk))
                chains.append(self.ln_ops(XR, xk, halves, 0))
                info.append((j, r0, xi, XR, xk))
            self.interleave(chains)
            for (j, r0, xi, XR, xk) in info:
                self.dma('sp', self.X1S[r0:r0 + 128, :], XR[:], [xk], ['X1S%d' % (r0 // 128)], 'st_x1_%d' % xi)
                self.to_fm(XR, xk, self.XTc, 'XTc', j)

    def ffn_chunk(self, s, l, qc):
        d = self.d
        t0 = qc * 512
        HT, XTc, CW, CBt, HALO = self.HT, self.XTc, self.CW, self.CBt, self.HALO
        for c in range(NFC):
            wi = self.ring('win', 3)
            WIN, wk = self.WINr[wi], 'WIN%d' % wi
            self.dma('sp', WIN.rearrange("p k n -> p (k n)"), self.FWINB[l, c], ['FWINB'], [wk], 'g_' + wk)
            PA, pak = self.bank6()
            PB, pbk = self.bank6()
            for k in range(8):
                self.mm(PA[:], WIN[:, k, 0:128], XTc[:, k, :], k == 0, k == 7, [wk, 'XTc'], [pak])
            for k in range(8):
                self.mm(PB[:], WIN[:, k, 128:256], XTc[:, k, :], k == 0, k == 7, [wk, 'XTc'], [pbk])
            ai = self.ring('ac', 3)
            AC, ak = self.ACr[ai], 'AC%d' % ai
            T1, tk = self.T1r[ai], 'T1%d' % ai
            GB, gk = self.GBr[ai], 'GB%d' % ai
            hk = 'HALO%d' % c
            self.copy(AC[:, 0:2], HALO[:, c, :], [hk], [ak + 'h'], eng='pool')
            self.copy(AC[:, 2:514], PA[:], [pak], [ak], eng='act')
            self.act(T1[:], PA[:], AF.Identity, [pak, 'CW', 'CBt'], [tk], bias=CBt[:, c:c + 1], scale=CW[:, c, 2:3])
            def f1(e, AC=AC, T1=T1, c=c):
                return e.scalar_tensor_tensor(out=T1[:], in0=AC[:, 1:513], scalar=CW[:, c, 1:2], in1=T1[:],
                                              op0=ALU.mult, op1=ALU.add)
            def f0(e, AC=AC, T1=T1, c=c):
                return e.scalar_tensor_tensor(out=T1[:], in0=AC[:, 0:512], scalar=CW[:, c, 0:1], in1=T1[:],
                                              op0=ALU.mult, op1=ALU.add)
            self.dve(f1, [ak, ak + 'h', tk, 'CW'], [tk])
            self.dve(f0, [ak, ak + 'h', tk, 'CW'], [tk])
            self.copy(HALO[:, c, :], AC[:, 512:514], [ak], [hk], eng='pool')
            self.act(GB[:], T1[:], AF.Gelu_apprx_tanh, [tk], [gk])
            self.tt(HT[:, c, :], GB[:], PB[:], ALU.mult, [gk, pbk], ['HT'])
        order = [6, 7, 0, 1, 2, 3, 4, 5]
        accs = [(self.banks[b], 'bank%d' % b) for b in order]
        for c in range(NFC):
            wi = self.ring('wout', 3)
            WO, wk = self.WOUTr[wi], 'WOUT%d' % wi
            self.dma('sp', WO, self.FWOUTB[l, c], ['FWOUTB'], [wk], 'g_' + wk)
            for j in range(4):
                for hf in range(2):
                    B, bk = accs[j * 2 + hf]
                    self.mm(B[:], HT[:, c, j * 128:(j + 1) * 128], WO[:, hf * 512:(hf + 1) * 512], c == 0, c == NFC - 1,
                            ['HT', wk], [bk])
        for pair in ((0, 1, 2, 3),):
            chains, info = [], []
            for j in pair:
                r0 = t0 + j * 128
                xi = self.ring('xr', 4)
                XR, xk = self.XRr[xi], 'XR%d' % xi
                self.dma('sp', XR[:], self.X1S[r0:r0 + 128, :], ['X1S%d' % (r0 // 128)], [xk], 'r_' + xk)
                halves = [(accs[j * 2 + hf][0][:], accs[j * 2 + hf][1]) for hf in range(2)]
                chains.append(self.ln_ops(XR, xk, halves, 2))
                info.append((j, r0, xi, XR, xk))
            self.interleave(chains)
            for (j, r0, xi, XR, xk) in info:
                if l == 0:
                    self.dma('sp', self.X2S[r0:r0 + 128, :], XR[:], [xk], ['X2S%d' % (r0 // 128)], 'st_x2_%d' % xi)
                    self.to_fm(XR, xk, self.XTc, 'XTc', j)
                else:
                    self.dma('sp', self.out[s, r0:r0 + 128, :], XR[:], [xk], [], 'st_out_%d' % xi)
        if l == 0:
            self.dma('sp', self.X2T.rearrange("(k p) t -> p k t", p=P)[:, :, t0:t0 + 512], XTc[:], ['XTc'], ['X2T%d' % qc], 'st_x2t')

    def build(self):
        d = self.d
        self.load_consts()
        if self.stop not in ('consts', 'proj', 'cmp', 'mem', 'l0_noffn'):
            self.precast_ffn_weights()
        for s in range(self.nseq):
            self.load_layer_consts(0)
            if self.stop == 'consts':
                break
            self.proj_pass(lambda qc: d['xT'][s].rearrange("(k p) t -> p k t", p=P)[:, :, qc * 512:(qc + 1) * 512], 'pool',
                           self.WA, 'WA', d['w0'].rearrange("(k p) n -> p k n", p=P), 8, [(8, 0), (9, 1), (10, 2)], self.l0_tok)
            if self.stop in ('proj', 'proj1', 'proj2', 'proj3', 'proj4', 'tok1', 'tok2'):
                break
            self.compress()
            if self.stop == 'cmp':
                break
            self.mem_kv(s, 'a_wmkv')
            if self.stop == 'mem':
                QC, qkey = self.load_q(0)
                self.mem_attn(QC, qkey)
                self.flush_attn()
                break
            self.dma('pool', self.WB[:], d['a_wout'].rearrange("(k p) n -> p k n", p=P), [], ['WB'], 'w_WB')
            for qc in range(4):
                if self.stop != 'l0_noattn':
                    self.nsa_chunk(qc)
                    if qc < 3:
                        self.prefetch_q(qc + 1)
                self.mixer_tail(s, 0, qc)
                if self.stop != 'l0_noffn':
                    self.ffn_chunk(s, 0, qc)
            if self.stop in ('l0', 'l0_noffn', 'l0_noattn'):
                continue
            self.load_layer_consts(1)
            x2src = lambda qc: self.X2T.rearrange("(k p) t -> p k t", p=P)[:, :, qc * 512:(qc + 1) * 512]
            self.proj_pass_l1(x2src)
            self.mem_kv(s, 'b_wmkv')
            self.dma('pool', self.WB[:], d['b_wout'].rearrange("(k p) n -> p k n", p=P), [], ['WB'], 'w_WB')
            for qc in range(4):
                self.moba_chunk(qc)
                if qc < 3:
                    self.prefetch_q(qc + 1)
                self.mixer_tail(s, 1, qc)
                self.ffn_chunk(s, 1, qc)
        self.S.emit(self.nc, self.st)
        return self.nc

    def proj_pass_l1(self, x2src):
        d = self.d
        S = self.S
        deps = ['X2T%d' % q for q in range(4)]

        def xs(qc):
            return x2src(qc)
        self._x2deps = deps
        self.proj_pass_dep(xs, 'sp', self.WA1, 'WA1', d['b_win'].rearrange("(k p) n -> p k n", p=P), 8, [], None)
        self.proj_pass_dep(xs, 'sp', self.WA2, 'WA2', d['skv'].rearrange("(k p) n -> p k n", p=P), 0,
                           [(g, g) for g in range(6)], self.l1_tok)
        KMEANF, KMEAN, PTK = self.KMEANF, self.KMEAN, self.PTK
        kreads = ['PTK%d_%d' % (g, q) for g in range(6) for q in range(4)]
        self.dve(lambda e: e.tensor_reduce(out=KMEANF[:], in_=PTK[:].rearrange("p g (b t) -> p g b t", t=256),
                                           axis=mybir.AxisListType.X, op=ALU.add), kreads, ['KMEANF'])
        self.dve(lambda e: e.tensor_scalar_mul(out=KMEAN[:], in0=KMEANF[:], scalar1=1.0 / 256.0), ['KMEANF'], ['KMEAN'])

    def proj_pass_dep(self, xsrc_fn, xeng, W, wkey, wsrc, ngq, kgroups, tok_fn):
        self.dma('pool', W, wsrc, [], [wkey], 'w_' + wkey)
        for qc in range(4):
            t0 = qc * 512
            ri = self.ring('xch', 2)
            XCH, xkey = self.XCHr[ri], 'XCH%d' % ri
            self.dma(xeng, XCH, xsrc_fn(qc), ['X2T%d' % qc], [xkey], 'x_' + xeng + xkey)
            for g in range(ngq):
                B, bk = self.bank()
                for k in range(8):
                    self.mm(B[:], W[:, k, g * 128:(g + 1) * 128], XCH[:, k, :], k == 0, k == 7, [wkey, xkey], [bk])
                self.scaled_copy(self.QST[:, g, :], B[:], 0.125, [bk], ['QST'])
            if ngq:
                self.dma('sp', self.QS[:, :, t0:t0 + 512].rearrange("g p t -> p g t"), self.QST[:, 0:8, :],
                         ['QST'], ['QS%d' % qc], 'qs_st')
            for (g, slot) in kgroups:
                B, bk = self.bank()
                for k in range(8):
                    self.mm(B[:], W[:, k, g * 128:(g + 1) * 128], XCH[:, k, :], k == 0, k == 7, [wkey, xkey], [bk])
                self.copy(self.PTK[:, slot, t0:t0 + 512], B[:], [bk], ['PTK%d_%d' % (slot, qc)])
            if tok_fn is not None:
                for j in range(4):
                    tok_fn(qc, j, XCH, xkey, W, wkey)


def build_program(nseq, debug=False, stop=None):
    kb = KB(nseq, debug, stop)
    kb.setup()
    with kb.st:
        nc = kb.build()
    return nc


_CACHE = {}


def kernel(**inputs):
    x = np.ascontiguousarray(np.asarray(inputs['x'], np.float32))
    mem = np.ascontiguousarray(np.asarray(inputs['mem'], np.float32))
    B = x.shape[0]
    nseq = B // NCORES
    tabs = _host_tables(inputs['rel_bias'])
    wts = _host_weights(inputs)
    shared = {}
    shared.update(tabs)
    shared.update(wts)
    for k, shp in INPUT_SHAPES.items():
        assert list(shared[k].shape) == shp, (k, shared[k].shape, shp)
    xT = np.ascontiguousarray(x.transpose(0, 2, 1))
    memT = np.ascontiguousarray(mem.transpose(0, 2, 1))
    in_maps = []
    for c in range(NCORES):
        m = dict(shared)
        m['x_tok'] = x[c * nseq:(c + 1) * nseq]
        m['xT'] = xT[c * nseq:(c + 1) * nseq]
        m['memT'] = memT[c * nseq:(c + 1) * nseq]
        in_maps.append(m)
    nc = build_program(nseq)
    res = run_bass_kernel_spmd(nc, in_maps, core_ids=list(range(NCORES)))
    out = np.concatenate([np.asarray(r['out']) for r in res.results], axis=0)
    return out.astype(np.float32)
```
